# Optimizing a Trainium2 kernel written in Bass

```python
import math
import jax, jax.numpy as jnp
from jax import lax
import numpy as np

D_MODEL = 1024
BATCH = 32
SEQ = 2048
DEPTH = 2
DEC_BATCH = 16
DEC_SEQ = 2048
PAST_LEN = 128

M_HEADS = 4
D_M = D_MODEL
M_DH = D_M // M_HEADS
M_CHUNK = 64
CONV_K = 5
FORGET_BIAS = 3.0
A_HEADS = 8
A_KV_HEADS = 2
A_DH = 64
D_A = A_HEADS * A_DH
A_KV = A_KV_HEADS * A_DH
WINDOW = 128
A_BLOCK = WINDOW
D_C = D_MODEL // 2
C_GROUPS = 4
C_CHUNK = 128
D_FF = 4 * D_MODEL
N_BRANCH = 3
ALPHA = (2 * DEPTH) ** 0.25
BETA = (8 * DEPTH) ** -0.25
LN_EPS = 1e-5

OFF_MX = 0
OFF_MZ = OFF_MX + D_M
OFF_MG = OFF_MZ + D_M
OFF_AQ = OFF_MG + 4 * M_HEADS
OFF_AK = OFF_AQ + D_A
OFF_AV = OFF_AK + A_KV
OFF_C = OFF_AV + A_KV
OFF_G = OFF_C + 2 * D_C
N_IN = OFF_G + N_BRANCH * D_MODEL

kernel_name = 'hybrid_bidir_mlstm_swa_sgu_encoder'


def layer_norm(x, w, b):
    xf = x.astype(jnp.float32)
    mu = jnp.mean(xf, -1, keepdims=True)
    var = jnp.mean(jnp.square(xf - mu), -1, keepdims=True)
    return ((xf - mu) * lax.rsqrt(var + LN_EPS)).astype(x.dtype) * w + b


def centred_dwconv(x, w, b):
    pad = CONV_K // 2
    y = lax.conv_general_dilated(x, w[:, None, :], window_strides=(1,), padding=[(pad, pad)],
                                 dimension_numbers=('NWC', 'WIO', 'NWC'),
                                 feature_group_count=x.shape[-1])
    return y + b


def mlstm_scan(q, k, v, ig, fg):
    B, H, S, dh = q.shape
    L = M_CHUNK
    nC = S // L
    logf = jax.nn.log_sigmoid(fg)

    def to_chunks(t):
        return jnp.moveaxis(t.reshape(B, H, nC, L, *t.shape[3:]), 2, 0)

    xs = (to_chunks(q), to_chunks(k), to_chunks(v), to_chunks(ig), to_chunks(logf))
    lower = jnp.tril(jnp.ones((L, L), dtype=bool))

    def step(carry, inp):
        C, n, m = carry
        qj, kj, vj, ij, fj = inp
        a = jnp.cumsum(fj, axis=-1)
        A = a[..., -1]
        Dm = a[..., :, None] - a[..., None, :] + ij[..., None, :]
        Dm = jnp.where(lower, Dm, -jnp.inf)
        inter = a + m[..., None]
        m_row = jnp.maximum(inter, jnp.max(Dm, -1))
        w_intra = jnp.exp(Dm - m_row[..., None])
        w_inter = jnp.exp(inter - m_row)
        s = jnp.einsum('bhld,bhsd->bhls', qj, kj) * w_intra
        num = (jnp.einsum('bhls,bhsd->bhld', s, vj)
               + w_inter[..., None] * jnp.einsum('bhvk,bhlk->bhlv', C, qj))
        den = jnp.sum(s, -1) + w_inter * jnp.einsum('bhk,bhlk->bhl', n, qj)
        h = num / jnp.maximum(jnp.abs(den), jnp.exp(-m_row))[..., None]
        g = A[..., None] - a + ij
        m_new = jnp.maximum(A + m, jnp.max(g, -1))
        wg = jnp.exp(g - m_new[..., None])
        decay = jnp.exp(A + m - m_new)
        C_new = decay[..., None, None] * C + jnp.einsum('bhs,bhsv,bhsk->bhvk', wg, vj, kj)
        n_new = decay[..., None] * n + jnp.einsum('bhs,bhsk->bhk', wg, kj)
        return (C_new, n_new, m_new), h

    init = (jnp.zeros((B, H, dh, dh), jnp.float32), jnp.zeros((B, H, dh), jnp.float32),
            jnp.zeros((B, H), jnp.float32))
    _, hc = lax.scan(step, init, xs)
    return jnp.moveaxis(hc, 0, 2).reshape(B, H, S, dh)


def mlstm_branch(xm, zm, gates, conv_w, conv_b, wq, wk, wv, norm_w):
    B, S, _ = xm.shape
    xc = jax.nn.silu(centred_dwconv(xm, conv_w, conv_b))
    heads = lambda t: t.reshape(B, S, M_HEADS, M_DH)
    f32 = jnp.float32
    q = jnp.einsum('bshd,hde->bhse', heads(xc), wq).astype(f32)
    k = (jnp.einsum('bshd,hde->bhse', heads(xc), wk) * (M_DH ** -0.5)).astype(f32)
    v = jnp.einsum('bshd,hde->bhse', heads(xm), wv).astype(f32)
    g = jnp.moveaxis(gates.astype(f32).reshape(B, S, 4, M_HEADS), 1, -1)
    h_fwd = mlstm_scan(q, k, v, g[:, 0], g[:, 1])
    rev = lambda t: jnp.flip(t, axis=2)
    h_bwd = rev(mlstm_scan(rev(q), rev(k), rev(v), rev(g[:, 2]), rev(g[:, 3])))
    h = h_fwd + h_bwd
    mu = jnp.mean(h, -1, keepdims=True)
    var = jnp.mean(jnp.square(h - mu), -1, keepdims=True)
    hn = (h - mu) * lax.rsqrt(var + LN_EPS)
    hn = jnp.moveaxis(hn, 1, 2).reshape(B, S, D_M).astype(xm.dtype) * norm_w
    return hn * jax.nn.sigmoid(zm)


def window_attention(q, k, v, sink):
    B, S = q.shape[0], q.shape[1]
    Q = A_BLOCK
    nB = S // Q
    G, R = A_KV_HEADS, A_HEADS // A_KV_HEADS
    f32 = jnp.float32
    qb = q.reshape(B, nB, Q, G, R, A_DH)
    pad = ((0, 0), (Q, Q), (0, 0), (0, 0))
    kp = jnp.pad(k, pad)
    vp = jnp.pad(v, pad)
    idx = jnp.arange(nB)[:, None] * Q + jnp.arange(3 * Q)[None, :]
    kb = kp[:, idx]
    vb = vp[:, idx]
    s = jnp.einsum('bnqgrd,bnkgd->bgrnqk', qb, kb).astype(f32) * (A_DH ** -0.5)
    qpos = jnp.arange(S).reshape(nB, Q)
    kpos = idx - Q
    dist_i = jnp.abs(qpos[:, :, None] - kpos[:, None, :])
    valid = (dist_i <= WINDOW) & (kpos[:, None, :] >= 0) & (kpos[:, None, :] < S)
    slopes = jnp.exp2(-8.0 * (jnp.arange(A_HEADS, dtype=f32) + 1.0) / A_HEADS).reshape(G, R)
    s = s - slopes[:, :, None, None, None] * dist_i.astype(f32)
    s = jnp.where(valid, s, -jnp.inf)
    sink_f = sink.astype(f32).reshape(G, R)[:, :, None, None, None]
    mx = jnp.maximum(jnp.max(s, -1, keepdims=True), sink_f)
    p = jnp.exp(s - mx)
    p = p / (jnp.sum(p, -1, keepdims=True) + jnp.exp(sink_f - mx))
    o = jnp.einsum('bgrnqk,bnkgd->bnqgrd', p.astype(v.dtype), vb)
    return o.reshape(B, S, D_A)


def spatial_gating(uv, ln_w, ln_b, ws, bs):
    B, S, _ = uv.shape
    uv = jax.nn.gelu(uv)
    u = uv[..., :D_C]
    v = layer_norm(uv[..., D_C:], ln_w, ln_b)
    nC = S // C_CHUNK
    vg = v.reshape(B, nC, C_CHUNK, C_GROUPS, D_C // C_GROUPS)
    vs = jnp.einsum('gts,bnsgc->bntgc', ws, vg) + bs.T[:, :, None]
    return u * vs.reshape(B, S, D_C)


def token_mixers(h, p, l):
    B, S, _ = h.shape
    proj = h @ p['w_in'][l] + p['b_in'][l]
    y_m = mlstm_branch(proj[..., OFF_MX:OFF_MZ], proj[..., OFF_MZ:OFF_MG], proj[..., OFF_MG:OFF_AQ],
                       p['m_conv_w'][l], p['m_conv_b'][l], p['m_wq'][l], p['m_wk'][l],
                       p['m_wv'][l], p['m_norm_w'][l])
    q = proj[..., OFF_AQ:OFF_AK].reshape(B, S, A_HEADS, A_DH)
    k = proj[..., OFF_AK:OFF_AV].reshape(B, S, A_KV_HEADS, A_DH)
    v = proj[..., OFF_AV:OFF_C].reshape(B, S, A_KV_HEADS, A_DH)
    y_a = window_attention(q, k, v, p['a_sink'][l])
    y_c = spatial_gating(proj[..., OFF_C:OFF_G], p['c_ln_w'][l], p['c_ln_b'][l],
                         p['c_ws'][l], p['c_bs'][l])
    g = jax.nn.sigmoid(proj[..., OFF_G:]).reshape(B, S, N_BRANCH, D_MODEL)
    merged = (g[..., 0, :] * (y_m @ p['p_m'][l])
              + g[..., 1, :] * (y_a @ p['p_a'][l])
              + g[..., 2, :] * (y_c @ p['p_c'][l]))
    return merged @ p['w_out'][l]


def encoder_trunk(x, c, p):
    for l in range(DEPTH):
        mod = jax.nn.silu(c) @ p['ada_w'][l] + p['ada_b'][l]
        sh1, sc1, g1, sh2, sc2, g2 = jnp.split(mod[:, None, :], 6, axis=-1)
        mix = token_mixers(x * (1 + sc1) + sh1, p, l)
        x = layer_norm(ALPHA * x + (1 + g1) * mix, p['ln1_w'][l], p['ln1_b'][l])
        hid = jnp.square(jax.nn.relu((x * (1 + sc2) + sh2) @ p['mlp_w1'][l] + p['mlp_b1'][l]))
        ff = hid @ p['mlp_w2'][l] + p['mlp_b2'][l]
        x = layer_norm(ALPHA * x + (1 + g2) * ff, p['ln2_w'][l], p['ln2_b'][l])
    return x


def setup_inputs(seed: int = 0) -> dict:
    key = jax.random.key(seed)
    ks = iter(jax.random.split(key, 40))
    nrm = lambda shape, scale: jax.random.normal(next(ks), shape, jnp.float32) * scale
    L = DEPTH
    f_cols = np.concatenate([OFF_MG + M_HEADS + np.arange(M_HEADS),
                             OFF_MG + 3 * M_HEADS + np.arange(M_HEADS)])
    b_in = nrm((L, N_IN), 0.02).at[:, f_cols].add(FORGET_BIAS)
    return {
        'x_prompt': nrm((BATCH, SEQ, D_MODEL), 1.0),
        'x_sample': nrm((DEC_BATCH, DEC_SEQ, D_MODEL), 1.0),
        'c_prompt': nrm((BATCH, D_MODEL), 1.0),
        'c_sample': nrm((DEC_BATCH, D_MODEL), 1.0),
        'ada_w': nrm((L, D_MODEL, 6 * D_MODEL), 0.2 * D_MODEL ** -0.5),
        'ada_b': nrm((L, 6 * D_MODEL), 0.02),
        'w_in': nrm((L, D_MODEL, N_IN), D_MODEL ** -0.5),
        'b_in': b_in,
        'm_conv_w': nrm((L, CONV_K, D_M), CONV_K ** -0.5),
        'm_conv_b': nrm((L, D_M), 0.02),
        'm_wq': nrm((L, M_HEADS, M_DH, M_DH), M_DH ** -0.5),
        'm_wk': nrm((L, M_HEADS, M_DH, M_DH), M_DH ** -0.5),
        'm_wv': nrm((L, M_HEADS, M_DH, M_DH), M_DH ** -0.5),
        'm_norm_w': 1.0 + nrm((L, D_M), 0.02),
        'a_sink': nrm((L, A_HEADS), 0.5),
        'c_ln_w': 1.0 + nrm((L, D_C), 0.02),
        'c_ln_b': nrm((L, D_C), 0.02),
        'c_ws': nrm((L, C_GROUPS, C_CHUNK, C_CHUNK), C_CHUNK ** -0.5),
        'c_bs': 1.0 + nrm((L, C_GROUPS, C_CHUNK), 0.02),
        'p_m': nrm((L, D_M, D_MODEL), BETA * D_M ** -0.5),
        'p_a': nrm((L, D_A, D_MODEL), BETA * D_A ** -0.5),
        'p_c': nrm((L, D_C, D_MODEL), BETA * D_C ** -0.5),
        'w_out': nrm((L, D_MODEL, D_MODEL), BETA * D_MODEL ** -0.5),
        'ln1_w': 1.0 + nrm((L, D_MODEL), 0.02),
        'ln1_b': nrm((L, D_MODEL), 0.02),
        'mlp_w1': nrm((L, D_MODEL, D_FF), D_MODEL ** -0.5),
        'mlp_b1': nrm((L, D_FF), 0.02),
        'mlp_w2': nrm((L, D_FF, D_MODEL), BETA * D_FF ** -0.5),
        'mlp_b2': nrm((L, D_MODEL), 0.02),
        'ln2_w': 1.0 + nrm((L, D_MODEL), 0.02),
        'ln2_b': nrm((L, D_MODEL), 0.02),
    }


def reference(x_prompt, x_sample, c_prompt, c_sample, ada_w, ada_b, w_in, b_in, m_conv_w, m_conv_b,
              m_wq, m_wk, m_wv, m_norm_w, a_sink, c_ln_w, c_ln_b, c_ws, c_bs, p_m, p_a, p_c, w_out,
              ln1_w, ln1_b, mlp_w1, mlp_b1, mlp_w2, mlp_b2, ln2_w, ln2_b):
    params = dict(ada_w=ada_w, ada_b=ada_b, w_in=w_in, b_in=b_in, m_conv_w=m_conv_w,
                  m_conv_b=m_conv_b, m_wq=m_wq, m_wk=m_wk, m_wv=m_wv, m_norm_w=m_norm_w,
                  a_sink=a_sink, c_ln_w=c_ln_w, c_ln_b=c_ln_b, c_ws=c_ws, c_bs=c_bs,
                  p_m=p_m, p_a=p_a, p_c=p_c, w_out=w_out, ln1_w=ln1_w, ln1_b=ln1_b,
                  mlp_w1=mlp_w1, mlp_b1=mlp_b1, mlp_w2=mlp_w2, mlp_b2=mlp_b2,
                  ln2_w=ln2_w, ln2_b=ln2_b)
    y_prompt = encoder_trunk(x_prompt, c_prompt, params)
    y_sample = encoder_trunk(x_sample, c_sample, params)
    return (y_prompt, y_sample)
```

```python
import contextlib
import numpy as np
import concourse.bass as bass
import concourse.mybir as mybir
from concourse.bass_utils import run_bass_kernel_spmd

F32 = mybir.dt.float32
BF16 = mybir.dt.bfloat16
AF = mybir.ActivationFunctionType
ALU = mybir.AluOpType

D = 1024
KD = 8
DEPTH = 2
NCORES = 8
M_HEADS = 4
M_DH = 256
A_HEADS = 8
A_DH = 64
D_A = 512
D_C = 512
D_FF = 4096
OFF_MX = 0
OFF_MZ = 1024
OFF_MG = 2048
OFF_AQ = 2064
OFF_AK = 2576
OFF_AV = 2704
OFF_C = 2832
OFF_G = 3856
N_IN = 6928
ALPHA = (2 * DEPTH) ** 0.25
LN_EPS = 1e-5
EPS_P = LN_EPS / (ALPHA * ALPHA)
NEG = -240000.0

ENGS = ("pe", "act", "dve", "pool", "sp")

VEC = {}
_o = 0
for _n, _k in (("b_xm", 8), ("b_zm", 8), ("b_qa", 4), ("b_ka", 4), ("b_u", 4), ("b_g", 24), ("conv_w", 40),
               ("conv_b", 8), ("norm_w", 8), ("ln1_w", 8), ("ln1_b", 8), ("ln2_w", 8), ("ln2_b", 8),
               ("b1", 32), ("b2", 8), ("ada_b", 48)):
    VEC[_n] = _o
    _o += _k
NV = _o
ROW = {"b_gv": 0, "b_vc": 144, "c_ln_w": 656, "c_ln_b": 1168, "a_sink": 1680}
NR = 1688


class Sched:
    def __init__(self):
        self.ops = []
        self.last_writer = {}
        self.readers = {}
        self.eng_ops = {e: [] for e in ENGS}
        self.last_dma = {}

    def op(self, eng, fn, reads=(), writes=(), dma=None, extra_deps=(), soft_deps=()):
        idx = len(self.ops)
        deps = set(extra_deps) | set(soft_deps)
        for k in reads:
            w = self.last_writer.get(k)
            if w is not None:
                deps.add(w)
        for k in writes:
            w = self.last_writer.get(k)
            if w is not None:
                deps.add(w)
            for r in self.readers.get(k, ()):
                deps.add(r)
        deps.discard(idx)
        if eng == "pe" and dma is None:
            deps = {d for d in deps if self.ops[d]["eng"] != "pe" or self.ops[d]["dma"] is not None}
        o = dict(eng=eng, fn=fn, deps=deps, dma=dma, signal=False, cnt=None, semkey=None, soft=set(soft_deps))
        self.ops.append(o)
        self.eng_ops[eng].append(idx)
        for k in reads:
            self.readers.setdefault(k, []).append(idx)
        for k in writes:
            self.last_writer[k] = idx
            self.readers[k] = []
        if dma is not None:
            self.last_dma[dma] = idx
        return idx

    def I(self, eng, name, *args, reads=(), writes=(), dma=None, **kw):
        return self.op(eng, (lambda h, name=name, args=args, kw=kw: getattr(h, name)(*args, **kw)), reads=reads, writes=writes, dma=dma)

    def barrier(self, final=False):
        lasts = set()
        for e in ENGS:
            for idx in reversed(self.eng_ops[e]):
                if self.ops[idx]["dma"] is None and self.ops[idx]["fn"] is not None:
                    lasts.add(idx)
                    break
        for k, idx in self.last_dma.items():
            if final or not k.startswith("wc_"):
                lasts.add(idx)
        for e in ENGS:
            self.op(e, None, extra_deps=lasts)
        keep = {k: v for k, v in self.last_writer.items()
                if self.ops[v]["dma"] is not None and self.ops[v]["dma"].startswith("wc_")}
        self.last_writer = keep
        self.readers = {}

    def emit(self, nc):
        ops = self.ops
        for o in ops:
            for d in o["deps"]:
                ops[d]["signal"] = True
            if o["dma"] is not None:
                o["signal"] = True
        counters = {}
        for e in ENGS:
            for idx in self.eng_ops[e]:
                o = ops[idx]
                if not o["signal"]:
                    continue
                if o["dma"] is not None:
                    key = ("dma", o["dma"])
                    counters[key] = counters.get(key, 0) + 16
                else:
                    key = ("eng", e)
                    counters[key] = counters.get(key, 0) + 1
                o["semkey"] = key
                o["cnt"] = counters[key]
        semkeys = list(counters.keys())
        self.n_sems = len(semkeys)
        print("n_sems", len(semkeys), "n_ops", len(ops))
        sems = {}
        with contextlib.ExitStack() as stack:
            for i, k in enumerate(semkeys):
                sems[k] = stack.enter_context(nc.semaphore("s%d" % i))
            block = stack.enter_context(nc.Block())

            def run_engine(e, handle):
                waited = {}
                for idx in self.eng_ops[e]:
                    o = ops[idx]
                    need = {}
                    for d in o["deps"]:
                        po = ops[d]
                        k = po["semkey"]
                        c_ = po["cnt"]
                        if po["dma"] is not None and po["dma"].startswith("wc_") and d not in o["soft"]:
                            c_ = counters[k]
                        if c_ > need.get(k, 0):
                            need[k] = c_
                    for k, v in need.items():
                        if waited.get(k, 0) < v:
                            handle.wait_ge(sems[k], v)
                            waited[k] = v
                    if o["fn"] is None:
                        continue
                    ins = o["fn"](handle)
                    if o["signal"]:
                        ins.then_inc(sems[o["semkey"]], 16 if o["dma"] is not None else 1)

            @block.tensor
            def _(h):
                run_engine("pe", h)

            @block.scalar
            def _(h):
                run_engine("act", h)

            @block.vector
            def _(h):
                run_engine("dve", h)

            @block.gpsimd
            def _(h):
                run_engine("pool", h)

            @block.sync
            def _(h):
                run_engine("sp", h)


class Arena:
    def __init__(self, ap, nwords):
        self.ap = ap
        self.n = nwords
        self.off = 0
        self.marks = []

    def f32(self, n):
        a = self.ap[:, self.off:self.off + n]
        self.off += n
        self.hw = max(getattr(self, "hw", 0), self.off)
        assert self.off <= self.n, "arena overflow %d > %d" % (self.off, self.n)
        return a

    def bf(self, n):
        w = (n + 1) // 2
        return self.f32(w).bitcast(BF16)[:, 0:n]

    def mark(self):
        self.marks.append(self.off)

    def release(self):
        self.off = self.marks.pop()


def build_program(NSEQ, S, NL=DEPTH, debug=(), stage=9):
    NT = S // 128
    NMT = S // 512
    nc = bass.Bass("TRN2", target_bir_lowering=False)
    dt = nc.dram_tensor

    x_d = dt("x", [NSEQ, S, D], F32, kind="ExternalInput").ap()
    cT_d = dt("cT", [128, KD * NSEQ], F32, kind="ExternalInput").ap()
    ada_d = dt("ada_w", [DEPTH, D, 6 * D], F32, kind="ExternalInput").ap()
    win_d = dt("w_in", [DEPTH, D, N_IN], F32, kind="ExternalInput").ap()
    wq_d = dt("m_wq", [DEPTH, 4, 256, 256], F32, kind="ExternalInput").ap()
    wk_d = dt("m_wk", [DEPTH, 4, 256, 256], F32, kind="ExternalInput").ap()
    wv_d = dt("m_wv", [DEPTH, 4, 256, 256], F32, kind="ExternalInput").ap()
    pm_d = dt("p_m", [DEPTH, D, D], F32, kind="ExternalInput").ap()
    pa_d = dt("p_a", [DEPTH, D_A, D], F32, kind="ExternalInput").ap()
    pc_d = dt("p_c", [DEPTH, D_C, D], F32, kind="ExternalInput").ap()
    wo_d = dt("w_out", [DEPTH, D, D], F32, kind="ExternalInput").ap()
    w1_d = dt("mlp_w1", [DEPTH, D, D_FF], F32, kind="ExternalInput").ap()
    w2_d = dt("mlp_w2", [DEPTH, D_FF, D], F32, kind="ExternalInput").ap()
    wsT_d = dt("wsT", [DEPTH, 128, 512], F32, kind="ExternalInput").ap()
    vec_d = dt("vec", [DEPTH, 128, NV], F32, kind="ExternalInput").ap()
    row_d = dt("rowv", [DEPTH, 1, NR], F32, kind="ExternalInput").ap()
    bs_d = dt("bsrow", [DEPTH, 1, 512], F32, kind="ExternalInput").ap()
    y_d = dt("y", [NSEQ, S, D], F32, kind="ExternalOutput").ap()
    dbg_d = {}
    for name, shape, dty in debug:
        dbg_d[name] = dt("dbg_" + name, list(shape), dty, kind="ExternalOutput").ap()

    def dump(name, src_ap, reads):
        if name in dbg_d:
            S_.barrier()
            S_.I("sp", "dma_start", out=dbg_d[name], in_=src_ap, reads=reads, writes=[("dbg", name)], dma="dbg_" + name)
            S_.barrier()

    NBLK_IN = 15
    wbin_d = dt("wb_in", [DEPTH, NBLK_IN, 128, 4096], BF16, kind="Internal").ap()
    wbqkv_d = dt("wb_qkv", [DEPTH, 3, 128, 2048], BF16, kind="Internal").ap()
    wbpm_d = dt("wb_pm", [DEPTH, 2, 128, 4096], BF16, kind="Internal").ap()
    wbwo_d = dt("wb_wo", [DEPTH, 2, 128, 4096], BF16, kind="Internal").ap()
    wbpa_d = dt("wb_pa", [DEPTH, 128, 4096], BF16, kind="Internal").ap()
    wbpc_d = dt("wb_pc", [DEPTH, 128, 4096], BF16, kind="Internal").ap()
    wb1_d = dt("wb_1", [DEPTH, 8, 128, 4096], BF16, kind="Internal").ap()
    wb2_d = dt("wb_2", [DEPTH, 8, 128, 4096], BF16, kind="Internal").ap()
    wbws_d = dt("wb_ws", [DEPTH, 128, 512], BF16, kind="Internal").ap()
    xT_d = dt("xT_scr", [NSEQ, 128, KD, S], F32, kind="Internal").ap()
    hf_d = dt("hfwd_scr", [NSEQ, S, D], F32, kind="Internal").ap()
    hn_d = dt("hnT_scr", [NSEQ, 128, KD, S], BF16, kind="Internal").ap()

    INBLK = {"xm0": (0, OFF_MX), "xm1": (1, OFF_MX + 512), "zm0": (2, OFF_MZ), "zm1": (3, OFF_MZ + 512),
             "qa": (4, OFF_AQ), "u": (5, OFF_C), "vc": (6, OFF_C + 512)}
    for i in range(6):
        INBLK["g%d" % i] = (7 + i, OFF_G + 512 * i)
    BLK_KA, BLK_GV = 13, 14

    S_ = Sched()
    op = S_.op
    I = S_.I

    with contextlib.ExitStack() as es:
        ARENA_W = 52736
        arena_t = es.enter_context(nc.sbuf_tensor("arena", [128, ARENA_W], F32))
        AR = Arena(arena_t[:, :], ARENA_W)
        PS = [es.enter_context(nc.psum_tensor("ps%d" % i, [128, 512], F32)) for i in range(8)]

        def ps(i):
            return PS[i][:, :]

        def psk(i):
            return ("ps", i)

        cast_blocks = []

        def v3(k):
            return lambda t: t.rearrange("p (k j) -> p k j", k=k)

        for l in range(DEPTH if stage >= 0 else 0):
            wsrc = win_d[l].rearrange("(k p) n -> p k n", p=128)
            for name, (b, c0) in INBLK.items():
                cast_blocks.append((wbin_d[l, b], 4096, [(v3(8), wsrc[:, :, c0:c0 + 512])]))
            pcs = []
            for g in range(2):
                for hf in range(2):
                    pcs.append(((lambda t, g=g, hf=hf: t.rearrange("p (k v j) -> p k v j", k=8, v=4)[:, :, g * 2 + hf, hf * 64:(hf + 1) * 64]),
                                wsrc[:, :, OFF_AK + 64 * g:OFF_AK + 64 * g + 64]))
            cast_blocks.append((wbin_d[l, BLK_KA], 4096, pcs, True))
            cast_blocks.append((wbin_d[l, BLK_GV, :, 0:1152], 1152, [
                ((lambda t: t[:, 0:1152].rearrange("p (k j) -> p k j", k=8)[:, :, 0:16]), wsrc[:, :, OFF_MG:OFF_MG + 16]),
                ((lambda t: t[:, 0:1152].rearrange("p (k j) -> p k j", k=8)[:, :, 16:144]), wsrc[:, :, OFF_AV:OFF_AV + 128])]))
            for i, wd in enumerate((wq_d, wk_d, wv_d)):
                cast_blocks.append((wbqkv_d[l, i], 2048, [((lambda t: t[:, 0:2048].rearrange("p (h c e) -> p h c e", h=4, c=2)), wd[l].rearrange("h (c p) e -> p h c e", p=128))]))
            for b in range(2):
                cast_blocks.append((wbpm_d[l, b], 4096, [(v3(8), pm_d[l].rearrange("(k p) n -> p k n", p=128)[:, :, b * 512:(b + 1) * 512])]))
                cast_blocks.append((wbwo_d[l, b], 4096, [(v3(8), wo_d[l].rearrange("(k p) n -> p k n", p=128)[:, :, b * 512:(b + 1) * 512])]))
            cast_blocks.append((wbpa_d[l], 4096, [(v3(4), pa_d[l].rearrange("(k p) n -> p k n", p=128))]))
            cast_blocks.append((wbpc_d[l], 4096, [(v3(4), pc_d[l].rearrange("(k p) n -> p k n", p=128))]))
            for b in range(8):
                cast_blocks.append((wb1_d[l, b], 4096, [(v3(8), w1_d[l].rearrange("(k p) n -> p k n", p=128)[:, :, b * 512:(b + 1) * 512])]))
            src2 = w2_d[l].rearrange("(k p) n -> p k n", p=128)
            for b in range(8):
                pcs = []
                for kk in range(4):
                    pcs.append(((lambda t, kk=kk: t.rearrange("p (k j) -> p k j", k=32)[:, kk * 8:(kk + 1) * 8, :]), src2[:, kk * 8:(kk + 1) * 8, b * 128:(b + 1) * 128]))
                cast_blocks.append((wb2_d[l, b], 4096, pcs))
            cast_blocks.append((wbws_d[l], 512, [((lambda t: t[:, 0:512]), wsT_d[l])]))

        ident_f = AR.f32(128)
        ident_b = AR.bf(128)
        ones_b = AR.bf(128)
        ones_f = AR.f32(128)
        Umask = AR.f32(128)
        Lmask = AR.f32(128)
        abias = AR.bf(3 * 2 * 512)
        tmpc = AR.f32(128)
        tmpc2 = AR.f32(128)
        I("pool", "memset", ident_f, 0.0, writes=["ident_f"])
        I("pool", "affine_select", out=ident_f, in_=ident_f, pattern=[[-1, 128]], compare_op=ALU.not_equal, fill=1.0,
                                             base=0, channel_multiplier=1, reads=["ident_f"], writes=["ident_f"])
        I("dve", "tensor_copy", out=ident_b, in_=ident_f, reads=["ident_f"], writes=["ident_b"])
        I("pool", "memset", ones_f, 1.0, writes=["ones_f"])
        I("pool", "memset", ones_b, 1.0, writes=["ones_b"])
        I("pool", "affine_select", out=Umask, in_=ones_f, pattern=[[1, 128]], compare_op=ALU.is_ge, fill=0.0,
                                             base=0, channel_multiplier=-1, reads=["ones_f"], writes=["Umask"])
        I("pool", "affine_select", out=Lmask, in_=ones_f, pattern=[[-1, 128]], compare_op=ALU.is_ge, fill=0.0,
                                             base=0, channel_multiplier=1, reads=["ones_f"], writes=["Lmask"])
        abv = abias.rearrange("p (o g r q) -> p o g r q", o=3, g=2, r=4)
        for oi, o in enumerate((-1, 0, 1)):
            I("pool", "iota", tmpc, pattern=[[1, 128]], base=-128 * o, channel_multiplier=-1,
                                             allow_small_or_imprecise_dtypes=True, writes=["tmpc"])
            for hh in range(8):
                g, r = hh // 4, hh % 4
                dst = abv[:, oi, g, r, :]
                cst = -(2.0 ** (2 - hh))
                if o == 0:
                    I("dve", "tensor_scalar", out=tmpc2, in0=tmpc, scalar1=cst, scalar2=None, op0=ALU.mult, reads=["tmpc"], writes=["tmpc2"])
                    I("dve", "scalar_tensor_tensor", out=dst, in0=tmpc, scalar=-cst, in1=tmpc2, op0=ALU.mult, op1=ALU.min, reads=["tmpc", "tmpc2"], writes=["abias"])
                else:
                    I("dve", "tensor_scalar", out=dst, in0=tmpc, scalar1=(cst if o == -1 else -cst), scalar2=None, op0=ALU.mult, reads=["tmpc"], writes=["abias"])
                if o == -1:
                    I("pool", "affine_select", out=dst, in_=dst, pattern=[[-1, 128]], compare_op=ALU.is_ge, fill=NEG,
                                                                  base=0, channel_multiplier=1, reads=["abias"], writes=["abias"])
                elif o == 1:
                    I("pool", "affine_select", out=dst, in_=dst, pattern=[[1, 128]], compare_op=ALU.is_ge, fill=NEG,
                                                                  base=0, channel_multiplier=-1, reads=["abias"], writes=["abias"])

        vec = AR.f32(NV)
        rowbc = AR.f32(NR)
        bsrow = AR.f32(512)
        esink = AR.f32(8)
        wsb = AR.bf(512)
        cT = AR.f32(KD * NSEQ)
        scT = AR.f32(KD * NSEQ)
        modT = AR.f32(48 * NSEQ)
        derv = AR.f32(7 * KD * NSEQ)
        modv = modT.rearrange("p (c s) -> p c s", s=NSEQ)
        dvv = derv.rearrange("p (i k s) -> p i k s", i=7, k=KD)
        A1, B1, S1, W2P, B2P, S2, X1B = range(7)
        kaT2 = AR.bf(4 * S)
        va = AR.bf(NT * 2 * 66)
        kaT2v = kaT2.rearrange("p (g s) -> p g s", g=4)
        vav = va.rearrange("p (t g j) -> p t g j", t=NT, g=2)

        def V(name, c=0, n=1):
            o = VEC[name] + c
            return vec[:, o:o + n]

        I("sp", "dma_start", out=cT, in_=cT_d, writes=["cT"], dma="cT")
        I("act", "activation", out=scT, in_=cT, func=AF.Silu, reads=["cT"], writes=["scT"])

        AR.mark()

        def run_casts():
            AR.mark()
            NB = 4
            Fst = [AR.f32(4096) for _ in range(NB)]
            Bst = [AR.bf(4096) for _ in range(NB)]
            n = len(cast_blocks)
            for bi in range(n + 2):
                if bi < n:
                    blk = cast_blocks[bi]
                    i = bi % NB
                    for vf, sap in blk[2]:
                        I("sp", "dma_start", out=vf(Fst[i]), in_=sap, writes=[("cF", i)], dma="cF%d" % i)
                bj = bi - 2
                if bj >= 0:
                    blk = cast_blocks[bj]
                    dst, W, pcs = blk[0], blk[1], blk[2]
                    i = bj % NB
                    F, B = Fst[i], Bst[i]
                    if len(blk) > 3:
                        I("pool", "memset", B, 0.0, writes=[("cB", i)])
                    for vf, sap in pcs:
                        if bj % 2 == 0:
                            I("act", "activation", out=vf(B), in_=vf(F), func=AF.Copy, reads=[("cF", i)], writes=[("cB", i)])
                        else:
                            I("dve", "tensor_copy", out=vf(B), in_=vf(F), reads=[("cF", i)], writes=[("cB", i)])
                    I("sp", "dma_start", out=dst, in_=B[:, 0:W], reads=[("cB", i)], writes=[("wb", bj)], dma="cB%d" % i)
            AR.release()
            S_.barrier()

        def layer_init(l):
            S_.barrier()
            I("sp", "dma_start", out=vec, in_=vec_d[l], writes=["vec"], dma="tab_vec")
            I("sp", "dma_start", out=rowbc, in_=row_d[l, 0].partition_broadcast(128), writes=["rowbc"], dma="tab_row")
            I("sp", "dma_start", out=bsrow[0:1, :], in_=bs_d[l], writes=["bsrow"], dma="tab_bs")
            I("sp", "dma_start", out=wsb, in_=wbws_d[l], reads=[("wbws", l)], writes=["wsb"], dma="tab_ws")
            I("act", "activation", out=esink, in_=rowbc[:, ROW["a_sink"]:ROW["a_sink"] + 8], func=AF.Exp, reads=["rowbc"], writes=["esink"])
            AR.mark()
            NSLOT = 4
            slots = [AR.f32(KD * 128) for _ in range(NSLOT)]
            scv = scT.rearrange("p (k s) -> p k s", k=KD)
            for c in range(48):
                sl = slots[c % NSLOT]
                slv = sl.rearrange("p (k j) -> p k j", k=KD)
                src = ada_d[l].rearrange("(k p) n -> p k n", p=128)[:, :, c * 128:(c + 1) * 128]
                I("sp", "dma_start", out=slv, in_=src, writes=[("adas", c % NSLOT)], dma="adas%d" % (c % NSLOT))
                bank = c % 2
                for k in range(KD):
                    I("pe", "matmul", ps(bank)[:, 0:NSEQ], lhsT=slv[:, k, :], rhs=scv[:, k, :], start=(k == 0), stop=(k == KD - 1),
                       reads=[("adas", c % NSLOT), "scT"], writes=[psk(bank)])
                I("act", "activation", out=modv[:, c, :], in_=ps(bank)[:, 0:NSEQ], func=AF.Identity, bias=V("ada_b", c), scale=1.0,
                   reads=[psk(bank), "vec"], writes=["modT"])
            def bc(name):
                o = VEC[name]
                return vec[:, o:o + KD].rearrange("p (k o) -> p k o", o=1).to_broadcast([128, KD, NSEQ])
            sh1, sc1, g1, sh2, sc2, g2 = (modv[:, i * 8:(i + 1) * 8, :] for i in range(6))
            dv = lambda i: dvv[:, i, :, :]
            R, W = ["modT", "vec", "derv"], ["derv"]
            I("dve", "tensor_scalar", out=dv(A1), in0=sc1, scalar1=1.0, scalar2=None, op0=ALU.add, reads=R, writes=W)
            I("dve", "tensor_copy", out=dv(B1), in_=sh1, reads=R, writes=W)
            I("dve", "tensor_scalar", out=dv(S1), in0=g1, scalar1=1.0, scalar2=1.0 / ALPHA, op0=ALU.add, op1=ALU.mult, reads=R, writes=W)
            I("dve", "tensor_scalar", out=dv(S2), in0=g2, scalar1=1.0, scalar2=1.0 / ALPHA, op0=ALU.add, op1=ALU.mult, reads=R, writes=W)
            I("dve", "tensor_scalar", out=dv(W2P), in0=sc2, scalar1=1.0, scalar2=None, op0=ALU.add, reads=R, writes=W)
            I("dve", "tensor_tensor", out=dv(B2P), in0=dv(W2P), in1=bc("ln1_b"), op=ALU.mult, reads=R, writes=W)
            I("dve", "tensor_tensor", out=dv(B2P), in0=dv(B2P), in1=sh2, op=ALU.add, reads=R, writes=W)
            I("dve", "tensor_tensor", out=dv(W2P), in0=dv(W2P), in1=bc("ln1_w"), op=ALU.mult, reads=R, writes=W)
            I("dve", "tensor_tensor", out=dv(X1B), in0=dv(S2), in1=bc("b2"), op=ALU.mult, reads=R, writes=W)
            I("dve", "tensor_tensor", out=dv(X1B), in0=dv(X1B), in1=bc("ln1_b"), op=ALU.add, reads=R, writes=W)
            AR.release()

        def DV(i, k, s):
            return dvv[:, i, k, s:s + 1]

        def phase_m(l, s):
            S_.barrier()
            AR.mark()
            xmT = AR.bf(KD * (S + 4))
            xcT = AR.bf(KD * S)
            xmv = xmT.rearrange("p (k t) -> p k t", k=KD)
            xcv = xcT.rearrange("p (k t) -> p k t", k=KD)
            gates = AR.f32(NT * 16)
            lsq = AR.f32(NT * 8)
            aneg = AR.f32(NT * 8)
            Aneg = AR.f32(NT * 8)
            bsc = AR.f32(NT * 8)
            bsck = AR.f32(NT * 8)
            ena = AR.f32(NT * 8)
            eA = AR.f32(NT * 8)
            gtmp = AR.f32(NT * 8)
            gv3 = gates.rearrange("p (t j) -> p t j", j=16)
            T8 = lambda a: a.rearrange("p (t j) -> p t j", j=8)
            wqkv = AR.bf(3 * 2048)
            wqkvv = wqkv.rearrange("p (i h c e) -> p i h c e", i=3, h=4, c=2)
            I("sp", "dma_start", out=wqkv.rearrange("p (i n) -> p i n", i=3), in_=wbqkv_d[l].rearrange("i p n -> p i n"),
               reads=[("wbqkv", l)], writes=["wqkv"], dma="wqkv")
            I("pool", "memset", xmv[:, :, 0:2], 0.0, writes=["xmT"])
            I("pool", "memset", xmv[:, :, S + 2:S + 4], 0.0, writes=["xmT"])
            I("pool", "memset", vav[:, :, :, 64:66], 1.0, writes=["va"])

            AR.mark()
            wxm = [AR.bf(4096), AR.bf(4096)]
            wka = AR.bf(4096)
            wgv = AR.bf(1152)
            for b in range(2):
                I("sp", "dma_start", out=wxm[b], in_=wbin_d[l, b], reads=[("wbin", l, b)], writes=[("wxm", b)], dma="wpre%d" % b)
            I("sp", "dma_start", out=wka, in_=wbin_d[l, BLK_KA], reads=[("wbin", l, BLK_KA)], writes=["wka"], dma="wpreka")
            I("sp", "dma_start", out=wgv, in_=wbin_d[l, BLK_GV, :, 0:1152], reads=[("wbin", l, BLK_GV)], writes=["wgv"], dma="wpregv")
            wxmv = [w.rearrange("p (k j) -> p k j", k=8) for w in wxm]
            wkav = wka.rearrange("p (k g j) -> p k g j", k=8, g=4)
            wgvv = wgv.rearrange("p (k j) -> p k j", k=8)
            xT = AR.f32(KD * 512)
            xTv = xT.rearrange("p (k t) -> p k t", k=KD)
            hT = AR.bf(KD * 512)
            hTv = hT.rearrange("p (k t) -> p k t", k=KD)
            xtok = [AR.f32(1024), AR.f32(1024)] if l == 0 else None
            pbank = [0]

            def nextbank(lo=0, hi=4):
                b = lo + pbank[0] % (hi - lo)
                pbank[0] += 1
                return b

            for m in range(NMT):
                t0 = m * 512
                if l == 0:
                    for tt in range(4):
                        xb = xtok[tt % 2]
                        I("sp", "dma_start", out=xb, in_=x_d[s, t0 + tt * 128:t0 + (tt + 1) * 128, :],
                           writes=[("xtok", tt % 2)], dma="xtok%d" % (tt % 2))
                        for kq in range(2):
                            bank = nextbank()
                            for kk in range(4):
                                k = kq * 4 + kk
                                I("pe", "transpose", out=ps(bank)[:, kk * 128:(kk + 1) * 128], in_=xb[:, k * 128:(k + 1) * 128], identity=ident_f,
                                   reads=[("xtok", tt % 2), "ident_f"], writes=[psk(bank)])
                            eng = "dve" if kq == 0 else "act"
                            dst = xTv[:, kq * 4:(kq + 1) * 4, tt * 128:(tt + 1) * 128]
                            srcp = ps(bank).rearrange("p (k t) -> p k t", k=4)
                            if eng == "dve":
                                I("dve", "tensor_copy", out=dst, in_=srcp, reads=[psk(bank)], writes=["xT"])
                            else:
                                I("act", "activation", out=dst, in_=srcp, func=AF.Copy, reads=[psk(bank)], writes=["xT"])
                    I("sp", "dma_start", out=xT_d[s, :, :, t0:t0 + 512], in_=xTv, reads=["xT"], writes=[("xTd", s, m)], dma="xTst")
                else:
                    I("sp", "dma_start", out=xTv, in_=xT_d[s, :, :, t0:t0 + 512], reads=[("xTd", s, m)], writes=["xT"], dma="xTld")
                for k in range(KD):
                    I("act", "activation", out=hTv[:, k, :], in_=xTv[:, k, :], func=AF.Identity, scale=DV(A1, k, s), bias=DV(B1, k, s),
                       reads=["xT", "derv"], writes=["hT"])
                for c in range(8):
                    bank = nextbank()
                    w = wxmv[c // 4]
                    for k in range(KD):
                        I("pe", "matmul", ps(bank), lhsT=w[:, k, (c % 4) * 128:(c % 4 + 1) * 128], rhs=hTv[:, k, :], start=(k == 0), stop=(k == KD - 1),
                           reads=["hT", ("wxm", c // 4)], writes=[psk(bank)])
                    I("act", "activation", out=xmv[:, c, 2 + t0:2 + t0 + 512], in_=ps(bank), func=AF.Identity, bias=V("b_xm", c), scale=1.0,
                       reads=[psk(bank), "vec"], writes=["xmT"])
                for g in range(4):
                    bank = nextbank()
                    for k in range(KD):
                        I("pe", "matmul", ps(bank), lhsT=wkav[:, k, g, :], rhs=hTv[:, k, :], start=(k == 0), stop=(k == KD - 1),
                           reads=["hT", "wka"], writes=[psk(bank)])
                    I("act", "activation", out=kaT2v[:, g, t0:t0 + 512], in_=ps(bank), func=AF.Identity, bias=V("b_ka", g), scale=1.0,
                       reads=[psk(bank), "vec"], writes=["kaT2"])
                for tt in range(4):
                    t = m * 4 + tt
                    bank = 4 + tt % 2
                    for k in range(KD):
                        I("pe", "matmul", ps(bank)[:, 0:144], lhsT=hTv[:, k, tt * 128:(tt + 1) * 128], rhs=wgvv[:, k, :], start=(k == 0), stop=(k == KD - 1),
                           reads=["hT", "wgv"], writes=[psk(bank)])
                    I("dve", "tensor_tensor", out=gv3[:, t, :], in0=ps(bank)[:, 0:16], in1=rowbc[:, 0:16], op=ALU.add,
                       reads=[psk(bank), "rowbc"], writes=["gates"])
                    I("dve", "tensor_tensor", out=vav[:, t, :, 0:64], in0=ps(bank)[:, 16:144].rearrange("p (g j) -> p g j", g=2),
                                                                        in1=rowbc[:, 16:144].rearrange("p (g j) -> p g j", g=2), op=ALU.add,
                       reads=[psk(bank), "rowbc"], writes=["va"])
            AR.release()
            S_.barrier()
            dump("xmT", xmT, ["xmT"])
            dump("kaT2", kaT2, ["kaT2"])
            dump("va", va, ["va"])
            dump("gates", gates, ["gates"])
            if stage < 3:
                AR.release()
                return

            AR.mark()
            dgs = AR.bf(40 * 128)
            dgv = dgs.rearrange("p (c j) -> p c j", c=40)
            for cj in range(40):
                I("dve", "tensor_scalar", out=dgv[:, cj, :], in0=ident_f, scalar1=V("conv_w", cj), scalar2=None, op0=ALU.mult, reads=["ident_f", "vec"], writes=[("dgs", cj)])
            cb = 0
            for c in range(KD):
                for q in range(S // 512):
                    bank = cb % 4
                    cb += 1
                    for j in range(5):
                        I("pe", "matmul", ps(bank), lhsT=dgv[:, c * 5 + j, :], rhs=xmv[:, c, q * 512 + j:q * 512 + j + 512], start=(j == 0), stop=(j == 4),
                          reads=[("dgs", c * 5 + j), "xmT"], writes=[psk(bank)])
                    I("act", "activation", out=xcv[:, c, q * 512:(q + 1) * 512], in_=ps(bank), func=AF.Silu, bias=V("conv_b", c), scale=1.0, reads=[psk(bank), "vec"], writes=["xcT"])
            gvd = gates.rearrange("p (t d j) -> p t d j", d=2, j=8)
            l4 = lsq.rearrange("p (t d j) -> p t d j", d=2, j=4)
            I("act", "activation", out=l4, in_=gvd[:, :, :, 4:8], func=AF.Exp, scale=-1.0, reads=["gates"], writes=["lsq"])
            I("act", "activation", out=lsq, in_=lsq, func=AF.Ln, bias=1.0, scale=1.0, reads=["lsq"], writes=["lsq"])
            NTJ = NT * 8
            I("pe", "matmul", ps(6)[:, 0:NTJ], lhsT=Umask, rhs=lsq, start=True, stop=True, reads=["lsq", "Umask"], writes=[psk(6)])
            I("pe", "matmul", ps(6)[:, NTJ:2 * NTJ], lhsT=Lmask, rhs=lsq, start=True, stop=True, reads=["lsq", "Lmask"], writes=[psk(6)])
            I("pe", "matmul", ps(7)[:, 0:NTJ], lhsT=ones_f, rhs=lsq, start=True, stop=True, reads=["lsq", "ones_f"], writes=[psk(7)])
            a4 = aneg.rearrange("p (t d j) -> p t d j", d=2, j=4)
            pu = ps(6)[:, 0:NTJ].rearrange("p (t d j) -> p t d j", d=2, j=4)
            pl = ps(6)[:, NTJ:2 * NTJ].rearrange("p (t d j) -> p t d j", d=2, j=4)
            I("dve", "tensor_copy", out=a4[:, :, 0, :], in_=pu[:, :, 0, :], reads=[psk(6)], writes=["aneg"])
            I("dve", "tensor_copy", out=a4[:, :, 1, :], in_=pl[:, :, 1, :], reads=[psk(6)], writes=["aneg"])
            I("dve", "tensor_copy", out=Aneg, in_=ps(7)[:, 0:NTJ], reads=[psk(7)], writes=["Aneg"])
            g4 = gtmp.rearrange("p (t d j) -> p t d j", d=2, j=4)
            I("dve", "tensor_tensor", out=g4, in0=gvd[:, :, :, 0:4], in1=a4, op=ALU.add, reads=["gates", "aneg"], writes=["gtmp"])
            I("act", "activation", out=bsc, in_=gtmp, func=AF.Exp, reads=["gtmp"], writes=["bsc"])
            I("act", "activation", out=bsck, in_=bsc, func=AF.Copy, scale=1.0 / 16.0, reads=["bsc"], writes=["bsck"])
            I("act", "activation", out=ena, in_=aneg, func=AF.Exp, reads=["aneg"], writes=["ena"])
            I("act", "activation", out=eA, in_=Aneg, func=AF.Exp, scale=-1.0, reads=["Aneg"], writes=["eA"])
            AR.release()
            S_.barrier()
            dump("xcT", xcT, ["xcT"])
            dump("bsc", bsc, ["bsc"])
            dump("ena", ena, ["ena"])
            dump("eA", eA, ["eA"])
            if stage < 4:
                AR.release()
                return

            AR.mark()
            qT = AR.bf(KD * 512)
            kT = AR.bf(KD * 512)
            qTv = qT.rearrange("p (k t) -> p k t", k=KD)
            kTv = kT.rearrange("p (k t) -> p k t", k=KD)
            kraw = [AR.bf(1024) for _ in range(4)]
            vtok = [AR.bf(4 * 258) for _ in range(4)]
            ktl = [AR.bf(1024), AR.bf(1024)]
            Sp = [AR.bf(512), AR.bf(512)]
            CTb = AR.bf(4 * 2 * 258)
            Cv = CTb.rearrange("p (h c j) -> p h c j", h=4, c=2)
            rr = AR.f32(8)
            htile = [AR.f32(1024), AR.f32(1024)]
            hfl = [AR.f32(1024), AR.f32(1024)]
            hnt = AR.bf(1024)
            hnT = [AR.bf(1024), AR.bf(1024)]
            bnst = AR.f32(4 * 6)
            bnag = AR.f32(4 * 2)
            lnt = AR.f32(8)
            for i in range(4):
                I("pool", "memset", vtok[i].rearrange("p (h j) -> p h j", h=4)[:, :, 256:258], 1.0, writes=[("vtok", i)])
            ti = 0
            ucnt = 0
            for d in range(2):
                I("pool", "memset", CTb, 0.0, writes=[("CTb", hh_) for hh_ in range(4)])
                mask = Umask if d == 0 else Lmask
                mlist = range(NMT) if d == 0 else range(NMT - 1, -1, -1)
                for m in mlist:
                    t0 = m * 512
                    pcnt = 0
                    for which, dstv, sc in ((0, qTv, 1.0), (1, kTv, 1.0 / 16.0)):
                        for hh in range(4):
                            for ec in range(2):
                                bank = pcnt % 4
                                pcnt += 1
                                for dc in range(2):
                                    I("pe", "matmul", ps(bank), lhsT=wqkvv[:, which, hh, dc, ec * 128:(ec + 1) * 128], rhs=xcv[:, hh * 2 + dc, t0:t0 + 512], start=(dc == 0), stop=(dc == 1),
                                      reads=["wqkv", "xcT"], writes=[psk(bank)])
                                dd = dstv[:, hh * 2 + ec, :]
                                wk_ = "qT" if which == 0 else "kT"
                                if ec == 0:
                                    I("act", "activation", out=dd, in_=ps(bank), func=AF.Copy, scale=sc, reads=[psk(bank)], writes=[wk_])
                                else:
                                    I("dve", "tensor_scalar", out=dd, in0=ps(bank), scalar1=sc, scalar2=None, op0=ALU.mult, reads=[psk(bank)], writes=[wk_])
                    for tt in range(4):
                        g0 = (m * 4 + tt) * 128
                        for which in (1, 2):
                            src_ = xcv if which == 1 else xmv
                            off = g0 if which == 1 else g0 + 2
                            for hp in range(2):
                                bank = pcnt % 4
                                pcnt += 1
                                for hi in range(2):
                                    hh = hp * 2 + hi
                                    for dc in range(2):
                                        I("pe", "matmul", ps(bank)[:, hi * 256:(hi + 1) * 256], lhsT=src_[:, hh * 2 + dc, off:off + 128], rhs=wqkvv[:, which, hh, dc, :], start=(dc == 0), stop=(dc == 1),
                                          reads=["wqkv", "xcT" if which == 1 else "xmT"], writes=[psk(bank)])
                                if which == 1:
                                    I("act", "activation", out=kraw[tt][:, hp * 512:(hp + 1) * 512], in_=ps(bank), func=AF.Copy, reads=[psk(bank)], writes=[("kraw", tt)])
                                else:
                                    I("dve", "tensor_copy", out=vtok[tt].rearrange("p (h j) -> p h j", h=4)[:, hp * 2:hp * 2 + 2, 0:256], in_=ps(bank).rearrange("p (h e) -> p h e", h=2),
                                      reads=[psk(bank)], writes=[("vtok", tt)])
                    tlist = range(4) if d == 0 else range(3, -1, -1)
                    for tt in tlist:
                        t = m * 4 + tt
                        tc0 = tt * 128
                        g0 = t * 128
                        kb_, sb_ = ktl[ti % 2], Sp[ti % 2]
                        kk_, sk_ = ("ktl", ti % 2), ("Sp", ti % 2)
                        vk_ = ("vtok", tt)
                        hb_, hk_ = htile[ti % 2], ("htile", ti % 2)
                        fb_, fk_ = hfl[ti % 2], ("hfl", ti % 2)
                        nb_, nk_ = hnT[ti % 2], ("hnT", ti % 2)
                        par = ti % 2
                        ti += 1
                        kbv = kb_.rearrange("p (h e) -> p h e", h=4)
                        vbv = vtok[tt].rearrange("p (h j) -> p h j", h=4)
                        sbv = sb_.rearrange("p (h q) -> p h q", h=4)
                        krv = kraw[tt].rearrange("p (h e) -> p h e", h=4)
                        if d == 1:
                            I("sp", "dma_start", out=fb_, in_=hf_d[s, g0:g0 + 128, :], reads=[("hfd", s, t)], writes=[fk_], dma="hfl%d" % par)
                        for hh in range(4):
                            for ec in range(2):
                                I("pe", "matmul", ps(4)[:, hh * 128:(hh + 1) * 128], lhsT=kTv[:, hh * 2 + ec, tc0:tc0 + 128], rhs=qTv[:, hh * 2 + ec, tc0:tc0 + 128],
                                  start=(ec == 0), stop=(ec == 1), reads=["qT", "kT"], writes=[psk(4)])
                        for hh in range(4):
                            I("pool", "tensor_scalar", out=kbv[:, hh, :], in0=krv[:, hh, :], scalar1=T8(bsck)[:, t, d * 4 + hh:d * 4 + hh + 1], scalar2=1.0, op0=ALU.mult, op1=ALU.mult,
                              reads=[("kraw", tt), "bsck"], writes=[(kk_, hh)])
                        for hh in range(4):
                            I("dve", "scalar_tensor_tensor", out=sbv[:, hh, :], in0=ps(4)[:, hh * 128:(hh + 1) * 128], scalar=T8(bsc)[:, t, d * 4 + hh:d * 4 + hh + 1], in1=mask, op0=ALU.mult, op1=ALU.mult,
                              reads=[psk(4), "bsc", "Umask", "Lmask"], writes=[(sk_, hh)])
                        for hh in range(4):
                            nbank = hh
                            I("pe", "matmul", ps(nbank)[:, 0:257], lhsT=sbv[:, hh, :], rhs=vbv[:, hh, 0:257], start=True, stop=False, reads=[(sk_, hh), vk_], writes=[psk(nbank)])
                            for ec in range(2):
                                I("pe", "matmul", ps(nbank)[:, 0:257], lhsT=qTv[:, hh * 2 + ec, tc0:tc0 + 128], rhs=Cv[:, hh, ec, 0:257], start=False, stop=(ec == 1),
                                  reads=["qT", ("CTb", hh)], writes=[psk(nbank)])
                        for hh in range(4):
                            for ec in range(2):
                                ubank = 5 + ucnt % (3 if d == 0 else 2)
                                ucnt += 1
                                I("pe", "matmul", ps(ubank)[:, 0:257], lhsT=kbv[:, hh, ec * 128:(ec + 1) * 128], rhs=vbv[:, hh, 0:257], start=True, stop=False, reads=[(kk_, hh), vk_], writes=[psk(ubank)])
                                I("pe", "matmul", ps(ubank)[:, 0:257], lhsT=ident_b, rhs=Cv[:, hh, ec, 0:257], start=False, stop=True, reads=["ident_b", ("CTb", hh)], writes=[psk(ubank)])
                                if ec == 0:
                                    I("act", "activation", out=Cv[:, hh, ec, 0:257], in_=ps(ubank)[:, 0:257], func=AF.Copy, scale=T8(eA)[:, t, d * 4 + hh:d * 4 + hh + 1],
                                      reads=[psk(ubank), "eA"], writes=[("CTb", hh)])
                                else:
                                    I("dve", "tensor_scalar", out=Cv[:, hh, ec, 0:257], in0=ps(ubank)[:, 0:257], scalar1=T8(eA)[:, t, d * 4 + hh:d * 4 + hh + 1], scalar2=None, op0=ALU.mult,
                                      reads=[psk(ubank), "eA"], writes=[("CTb", hh)])
                        for hh in range(4):
                            nbank = hh
                            rc = rr[:, hh:hh + 1]
                            I("dve", "tensor_scalar", out=rc, in0=ps(nbank)[:, 256:257], scalar1=T8(ena)[:, t, d * 4 + hh:d * 4 + hh + 1], scalar2=None, op0=ALU.max,
                              reads=[psk(nbank), "ena"], writes=[("rr", hh)])
                            I("dve", "scalar_tensor_tensor", out=rc, in0=ps(nbank)[:, 256:257], scalar=-1.0, in1=rc, op0=ALU.mult, op1=ALU.max,
                              reads=[psk(nbank), ("rr", hh)], writes=[("rr", hh)])
                            I("dve", "reciprocal", out=rc, in_=rc, reads=[("rr", hh)], writes=[("rr", hh)])
                            hdst = hb_[:, hh * 256:(hh + 1) * 256]
                            if d == 0:
                                I("dve", "tensor_scalar", out=hdst, in0=ps(nbank)[:, 0:256], scalar1=rc, scalar2=None, op0=ALU.mult, reads=[psk(nbank), ("rr", hh)], writes=[hk_])
                            else:
                                I("dve", "scalar_tensor_tensor", out=hdst, in0=ps(nbank)[:, 0:256], scalar=rc, in1=fb_[:, hh * 256:(hh + 1) * 256], op0=ALU.mult, op1=ALU.add,
                                  reads=[psk(nbank), ("rr", hh), fk_], writes=[hk_])
                        if d == 0:
                            I("sp", "dma_start", out=hf_d[s, g0:g0 + 128, :], in_=hb_, reads=[hk_], writes=[("hfd", s, t)], dma="hfst%d" % par)
                        else:
                            for hh in range(4):
                                I("dve", "bn_stats", out=bnst[:, hh * 6:(hh + 1) * 6], in_=hb_[:, hh * 256:(hh + 1) * 256], reads=[hk_], writes=[("bnst", hh)])
                                I("dve", "bn_aggr", out=bnag[:, hh * 2:(hh + 1) * 2], in_=bnst[:, hh * 6:(hh + 1) * 6], reads=[("bnst", hh)], writes=["bnag"])
                            bv = bnag.rearrange("p (h j) -> p h j", j=2)
                            I("act", "activation", out=lnt[:, 0:4], in_=bv[:, :, 1], func=AF.Ln, bias=LN_EPS, scale=1.0, reads=["bnag"], writes=["lnt"])
                            I("act", "activation", out=lnt[:, 0:4], in_=lnt[:, 0:4], func=AF.Exp, scale=-0.5, reads=["lnt"], writes=["lnt"])
                            I("dve", "scalar_tensor_tensor", out=lnt[:, 4:8], in0=bv[:, :, 0], scalar=-1.0, in1=lnt[:, 0:4], op0=ALU.mult, op1=ALU.mult, reads=["bnag", "lnt"], writes=["lnt"])
                            for hh in range(4):
                                I("act", "activation", out=hnt[:, hh * 256:(hh + 1) * 256], in_=hb_[:, hh * 256:(hh + 1) * 256], func=AF.Identity, scale=lnt[:, hh:hh + 1], bias=lnt[:, 4 + hh:5 + hh],
                                  reads=[hk_, "lnt"], writes=["hnt"])
                            pst = ps(7).bitcast(BF16)
                            for k in range(KD):
                                I("pe", "transpose", out=pst[:, k * 128:(k + 1) * 128], in_=hnt[:, k * 128:(k + 1) * 128], identity=ident_b, reads=["hnt", "ident_b"], writes=[psk(7)])
                            nw = vec[:, VEC["norm_w"]:VEC["norm_w"] + 8].rearrange("p (k o) -> p k o", o=1).to_broadcast([128, 8, 128])
                            I("dve", "tensor_tensor", out=nb_.rearrange("p (k t) -> p k t", k=8), in0=pst.rearrange("p (k t) -> p k t", k=8), in1=nw, op=ALU.mult, reads=[psk(7), "vec"], writes=[nk_])
                            I("sp", "dma_start", out=hn_d[s, :, :, g0:g0 + 128], in_=nb_.rearrange("p (k t) -> p k t", k=8), reads=[nk_], writes=[("hnd", s, t)], dma="hnst%d" % par)
            AR.release()
            AR.release()
            dump("hnT", hn_d[s], [])
            dump("hf", hf_d[s], [])

        def phase_main(l, s, last):
            S_.barrier()
            AR.mark()
            NSLOT = 4
            wslot = [AR.bf(4096) for _ in range(NSLOT)]
            wctr = [0]

            def wload(src, key):
                i = wctr[0] % NSLOT
                wctr[0] += 1
                sl = wslot[i]
                I("sp", "dma_start", out=sl, in_=src, reads=[key], writes=[("wslot", i)], dma="wsl%d" % i)
                return sl, ("wslot", i)

            xT = AR.f32(KD * 512)
            xTv = xT.rearrange("p (k t) -> p k t", k=KD)
            hT = AR.bf(KD * 512)
            hTv = hT.rearrange("p (k t) -> p k t", k=KD)
            hid = AR.bf(32 * 512)
            hidv = hid.rearrange("p (k t) -> p k t", k=32)
            gTv = hidv[:, 0:24, :]
            szv = hidv[:, 24:32, :]
            ymT = AR.bf(KD * 512)
            ymv = ymT.rearrange("p (k t) -> p k t", k=KD)
            qaT = AR.bf(4 * 512)
            qav = qaT.rearrange("p (k t) -> p k t", k=4)
            uT = AR.bf(4 * 512)
            uv_ = uT.rearrange("p (k t) -> p k t", k=4)
            vln = [AR.bf(512) for _ in range(4)]
            vcf = [AR.f32(512) for _ in range(4)]
            pT = [AR.bf(3 * 512), AR.bf(3 * 512)]
            yat = AR.bf(512)
            yaT = AR.bf(4 * 512)
            yav = yaT.rearrange("p (k t) -> p k t", k=4)
            tmg = [AR.f32(512) for _ in range(3)]
            mg = AR.bf(KD * 512)
            mgv = mg.rearrange("p (k t) -> p k t", k=KD)
            zsq = AR.bf(KD * 512)
            zsv = zsq.rearrange("p (k t) -> p k t", k=KD)
            mean = AR.f32(512)
            rstd = AR.f32(512)
            stt = AR.f32(512)
            rl = [AR.f32(512), AR.f32(512)]
            sm8 = AR.f32(16)
            bnst = AR.f32(6)
            bnag = AR.f32(2)
            lnt = AR.f32(2)
            zsq_f = zsq.bitcast(F32)
            ytok = [zsq_f[:, 0:1024], zsq_f[:, 1024:2048]] if last else None
            ZK = [("zsq", k) for k in range(KD)]
            pb = [0]

            def nb(lo, hi):
                b = lo + pb[0] % (hi - lo)
                pb[0] += 1
                return b

            def layer_norm(zkey, wname, bname, extra_bias_idx, make_h2, after_chunk=None):
                for k in range(KD):
                    I("act", "activation", out=mgv[:, k, :], in_=xTv[:, k, :], func=AF.Copy, reads=[("xT", k)], writes=[("mg", k)])
                    I("pool", "tensor_tensor", out=zsv[:, k, :], in0=xTv[:, k, :], in1=xTv[:, k, :], op=ALU.mult, reads=[("xT", k)], writes=[("zsq", k)])
                for k in range(KD):
                    I("pe", "matmul", ps(0), lhsT=ones_b, rhs=mgv[:, k, :], start=(k == 0), stop=(k == KD - 1), reads=[("mg", k), "ones_b"], writes=[psk(0)])
                for k in range(KD):
                    I("pe", "matmul", ps(1), lhsT=ones_b, rhs=zsv[:, k, :], start=(k == 0), stop=(k == KD - 1), reads=[("zsq", k), "ones_b"], writes=[psk(1)])
                I("act", "activation", out=mean, in_=ps(0), func=AF.Copy, scale=1.0 / D, reads=[psk(0)], writes=["mean"])
                I("dve", "tensor_tensor", out=stt, in0=mean, in1=mean, op=ALU.mult, reads=["mean"], writes=["stt"])
                I("dve", "scalar_tensor_tensor", out=stt, in0=ps(1), scalar=1.0 / D, in1=stt, op0=ALU.mult, op1=ALU.subtract, reads=[psk(1), "stt"], writes=["stt"])
                I("dve", "tensor_scalar", out=stt, in0=stt, scalar1=0.0, scalar2=EPS_P, op0=ALU.max, op1=ALU.add, reads=["stt"], writes=["stt"])
                I("act", "activation", out=rstd, in_=stt, func=AF.Ln, reads=["stt"], writes=["rstd"])
                I("act", "activation", out=rstd, in_=rstd, func=AF.Exp, scale=-0.5, reads=["rstd"], writes=["rstd"])
                def tail(k):
                    I("dve", "tensor_tensor", out=xTv[:, k, :], in0=xTv[:, k, :], in1=rstd, op=ALU.mult, reads=[("xT", k), "rstd"], writes=[("xT", k)])
                    if make_h2:
                        I("act", "activation", out=hTv[:, k, :], in_=xTv[:, k, :], func=AF.Identity, scale=DV(W2P, k, s), bias=DV(B2P, k, s),
                           reads=[("xT", k), "derv"], writes=[("hT", k)])
                        I("act", "activation", out=xTv[:, k, :], in_=xTv[:, k, :], func=AF.Identity, scale=V(wname, k), bias=DV(X1B, k, s),
                           reads=[("xT", k), "derv", "vec"], writes=[("xT", k)])
                    else:
                        I("act", "activation", out=xTv[:, k, :], in_=xTv[:, k, :], func=AF.Identity, scale=V(wname, k), bias=V(bname, k),
                           reads=[("xT", k), "vec"], writes=[("xT", k)])
                    if after_chunk is not None:
                        after_chunk(k)
                for k in range(KD):
                    I("dve", "tensor_tensor", out=xTv[:, k, :], in0=xTv[:, k, :], in1=mean, op=ALU.subtract, reads=[("xT", k), "mean"], writes=[("xT", k)])
                    if k >= 1:
                        tail(k - 1)
                tail(KD - 1)

            for m in range(NMT):
                t0 = m * 512
                for k in range(KD):
                    I("sp", "dma_start", out=xTv[:, k, :], in_=xT_d[s, :, k, t0:t0 + 512], reads=[("xTd", s, m)], writes=[("xT", k)], dma="xTld%d" % k)
                I("sp", "dma_start", out=ymv, in_=hn_d[s, :, :, t0:t0 + 512], reads=[("hnd", s, m * 4 + i) for i in range(4)], writes=["ymT"], dma="hnld")
                for k in range(KD):
                    I("act", "activation", out=hTv[:, k, :], in_=xTv[:, k, :], func=AF.Identity, scale=DV(A1, k, s), bias=DV(B1, k, s),
                       reads=[("xT", k), "derv"], writes=[("hT", k)])
                HT = [("hT", k) for k in range(KD)]

                def proj_fm(blkname, nchunks, evac):
                    b, _ = INBLK[blkname]
                    sl, sk = wload(wbin_d[l, b], ("wbin", l, b))
                    slv = sl.rearrange("p (k j) -> p k j", k=8)
                    for c in range(nchunks):
                        bank = nb(0, 4)
                        for k in range(KD):
                            I("pe", "matmul", ps(bank), lhsT=slv[:, k, c * 128:(c + 1) * 128], rhs=hTv[:, k, :], start=(k == 0), stop=(k == KD - 1),
                               reads=[sk, ("hT", k)], writes=[psk(bank)])
                        evac(c, bank)

                for half in range(2):
                    def ev_zm(c, bank, half=half):
                        cc = half * 4 + c
                        I("act", "activation", out=szv[:, cc, :], in_=ps(bank), func=AF.Sigmoid, bias=V("b_zm", cc), scale=1.0, reads=[psk(bank), "vec"], writes=[("sz", cc), ("hid", 24 + cc)])
                    proj_fm("zm%d" % half, 4, ev_zm)

                def ev_qa(c, bank):
                    I("act", "activation", out=qav[:, c, :], in_=ps(bank), func=AF.Identity, bias=V("b_qa", c), scale=1.0, reads=[psk(bank), "vec"], writes=["qaT"])
                proj_fm("qa", 4, ev_qa)

                def ev_u(c, bank):
                    I("act", "activation", out=uv_[:, c, :], in_=ps(bank), func=AF.Gelu_apprx_tanh, bias=V("b_u", c), scale=1.0, reads=[psk(bank), "vec"], writes=[("uT", c)])
                proj_fm("u", 4, ev_u)

                for gi in range(6):
                    def ev_g(c, bank, gi=gi):
                        cc = gi * 4 + c
                        I("act", "activation", out=gTv[:, cc, :], in_=ps(bank), func=AF.Sigmoid, bias=V("b_g", cc), scale=1.0, reads=[psk(bank), "vec"], writes=[("gT", cc), ("hid", cc)])
                    proj_fm("g%d" % gi, 4, ev_g)

                for k in range(KD):
                    I("pool", "tensor_tensor", out=ymv[:, k, :], in0=ymv[:, k, :], in1=szv[:, k, :], op=ALU.mult, reads=["ymT", ("sz", k)], writes=["ymT"])

                slvc, skvc = wload(wbin_d[l, INBLK["vc"][0]], ("wbin", l, INBLK["vc"][0]))
                slvcv = slvc.rearrange("p (k j) -> p k j", k=8)
                wsv = wsb.rearrange("p (g t) -> p g t", g=4)
                for tt in range(4):
                    if stage < 6:
                        break
                    t = m * 4 + tt
                    c0 = tt * 128
                    vb, vbk = vln[tt], ("vln", tt)
                    vf, vfk = vcf[tt], ("vcf", tt)
                    vbank = 4 + tt % 2
                    for k in range(KD):
                        I("pe", "matmul", ps(vbank), lhsT=hTv[:, k, c0:c0 + 128], rhs=slvcv[:, k, :], start=(k == 0), stop=(k == KD - 1), reads=[skvc, ("hT", k)], writes=[psk(vbank)])
                    I("dve", "tensor_tensor", out=vf, in0=ps(vbank), in1=rowbc[:, ROW["b_vc"]:ROW["b_vc"] + 512], op=ALU.add, reads=[psk(vbank), "rowbc"], writes=[vfk])
                    I("act", "activation", out=vf, in_=vf, func=AF.Gelu_apprx_tanh, reads=[vfk], writes=[vfk])
                    I("dve", "bn_stats", out=bnst, in_=vf, reads=[vfk], writes=["bnst"])
                    I("dve", "bn_aggr", out=bnag, in_=bnst, reads=["bnst"], writes=["bnag"])
                    I("act", "activation", out=lnt[:, 0:1], in_=bnag[:, 1:2], func=AF.Ln, bias=LN_EPS, scale=1.0, reads=["bnag"], writes=["lnt"])
                    I("act", "activation", out=lnt[:, 0:1], in_=lnt[:, 0:1], func=AF.Exp, scale=-0.5, reads=["lnt"], writes=["lnt"])
                    I("dve", "scalar_tensor_tensor", out=lnt[:, 1:2], in0=bnag[:, 0:1], scalar=-1.0, in1=lnt[:, 0:1], op0=ALU.mult, op1=ALU.mult, reads=["bnag", "lnt"], writes=["lnt"])
                    I("act", "activation", out=vf, in_=vf, func=AF.Identity, scale=lnt[:, 0:1], bias=lnt[:, 1:2], reads=[vfk, "lnt"], writes=[vfk])
                    I("pool", "tensor_tensor", out=vf, in0=vf, in1=rowbc[:, ROW["c_ln_w"]:ROW["c_ln_w"] + 512], op=ALU.mult, reads=[vfk, "rowbc"], writes=[vfk])
                    I("pool", "tensor_tensor", out=vb, in0=vf, in1=rowbc[:, ROW["c_ln_b"]:ROW["c_ln_b"] + 512], op=ALU.add, reads=[vfk, "rowbc"], writes=[vbk])
                for tt in range(4):
                    t = m * 4 + tt
                    c0 = tt * 128
                    if stage < 7:
                        continue
                    offs = [o for o in (-1, 0, 1) if 0 <= t + o < NT]
                    for g in range(2):
                        pb_, pk_ = pT[g], ("pT", g)
                        pv = pb_.rearrange("p (o r q) -> p o r q", o=3, r=4)
                        for o in offs:
                            oi = o + 1
                            kt = t + o
                            bank = nb(0, 4)
                            I("pe", "matmul", ps(bank), lhsT=ident_b, rhs=abv[:, oi, g, :, :], start=True, stop=False,
                               reads=["abias", "ident_b"], writes=[psk(bank)])
                            for r in range(4):
                                hh = g * 4 + r
                                pr = (hh % 2) * 64
                                I("pe", "matmul", ps(bank)[:, r * 128:(r + 1) * 128], lhsT=kaT2v[:, g * 2 + hh % 2, kt * 128:(kt + 1) * 128],
                                  rhs=qav[:, hh // 2, c0:c0 + 128], start=False, stop=(r == 3), skip_group_check=True,
                                  reads=["kaT2", "qaT"], writes=[psk(bank)])
                            I("act", "activation", out=pv[:, oi, :, :], in_=ps(bank).rearrange("p (r q) -> p r q", r=4), func=AF.Exp, scale=0.125,
                               reads=[psk(bank)], writes=[pk_])
                        if stage < 7.5:
                            continue
                        obank = 6 + g
                        for r in range(4):
                            for j, o in enumerate(offs):
                                oi = o + 1
                                kt = t + o
                                I("pe", "matmul",
                                    ps(obank)[:, r * 66:r * 66 + 65], lhsT=pv[:, oi, r, :], rhs=vav[:, kt, g, 0:65], start=(j == 0), stop=(j == len(offs) - 1),
                                   reads=[pk_, "va"], writes=[psk(obank)])
                        po = ps(obank)[:, 0:264].rearrange("p (r j) -> p r j", r=4)
                        I("dve", "tensor_tensor", out=sm8[:, g * 4:(g + 1) * 4], in0=po[:, :, 64], in1=esink[:, g * 4:(g + 1) * 4], op=ALU.add, reads=[psk(obank), "esink"], writes=[("sm8", g)])
                        I("dve", "reciprocal", out=sm8[:, g * 4:(g + 1) * 4], in_=sm8[:, g * 4:(g + 1) * 4], reads=[("sm8", g)], writes=[("sm8", g)])
                        yv = yat.rearrange("p (r j) -> p r j", j=64)
                        I("dve", "tensor_tensor", out=yv[:, g * 4:(g + 1) * 4, :], in0=po[:, :, 0:64],
                                                                               in1=sm8[:, g * 4:(g + 1) * 4].rearrange("p (r o) -> p r o", o=1).to_broadcast([128, 4, 64]), op=ALU.mult,
                           reads=[psk(obank), ("sm8", g)], writes=["yat"])
                    if stage < 7.7:
                        continue
                    pst = ps(5).bitcast(BF16)
                    for k in range(4):
                        I("pe", "transpose", out=pst[:, 512 + k * 128:512 + (k + 1) * 128], in_=yat[:, k * 128:(k + 1) * 128], identity=ident_b,
                           reads=["yat", "ident_b"], writes=[psk(5)])
                    I("act", "activation", out=yav[:, :, c0:c0 + 128], in_=pst[:, 512:1024].rearrange("p (k t) -> p k t", k=4), func=AF.Copy, reads=[psk(5)], writes=["yaT"])

                for tt in range(4):
                    if stage < 6:
                        break
                    c0 = tt * 128
                    vb, vbk = vln[tt], ("vln", tt)
                    sbank = 4 + tt % 2
                    for gg in range(4):
                        I("pe", "matmul", ps(sbank)[:, gg * 128:(gg + 1) * 128], lhsT=vb[:, gg * 128:(gg + 1) * 128], rhs=wsv[:, gg, :], start=True, stop=False,
                           reads=[vbk, "wsb"], writes=[psk(sbank)])
                        I("pe", "matmul", ps(sbank)[:, gg * 128:(gg + 1) * 128], lhsT=ones_f[0:1, :], rhs=bsrow[0:1, gg * 128:(gg + 1) * 128], start=False, stop=True,
                           reads=["bsrow", "ones_f"], writes=[psk(sbank)])
                    I("dve", "tensor_tensor", out=uv_[:, :, c0:c0 + 128], in0=ps(sbank).rearrange("p (g t) -> p g t", g=4), in1=uv_[:, :, c0:c0 + 128], op=ALU.mult,
                       reads=[psk(sbank)] + [("uT", c) for c in range(4)], writes=[("uT", c) for c in range(4)])
                if m == 0:
                    dump("ymT", ymT, ["ymT"])
                    dump("yaT", yaT, ["yaT"])
                    dump("ycT", uT, [("uT", c) for c in range(4)])
                if stage < 8:
                    continue
                slpa, skpa = wload(wbpa_d[l], ("wbpa", l))
                slpc, skpc = wload(wbpc_d[l], ("wbpc", l))
                pav = slpa.rearrange("p (k j) -> p k j", k=4)
                pcv = slpc.rearrange("p (k j) -> p k j", k=4)
                for half in range(2):
                    slpm, skpm = wload(wbpm_d[l, half], ("wbpm", l, half))
                    pmv = slpm.rearrange("p (k j) -> p k j", k=8)
                    for cc in range(4):
                        n = half * 4 + cc
                        bb = 3 * (n % 2)
                        for k in range(KD):
                            I("pe", "matmul", ps(bb), lhsT=pmv[:, k, cc * 128:(cc + 1) * 128], rhs=ymv[:, k, :], start=(k == 0), stop=(k == KD - 1),
                               reads=[skpm, "ymT"], writes=[psk(bb)])
                        for k in range(4):
                            I("pe", "matmul", ps(bb + 1), lhsT=pav[:, k, n * 128:(n + 1) * 128], rhs=yav[:, k, :], start=(k == 0), stop=(k == 3), reads=[skpa, "yaT"], writes=[psk(bb + 1)])
                        for k in range(4):
                            I("pe", "matmul", ps(bb + 2), lhsT=pcv[:, k, n * 128:(n + 1) * 128], rhs=uv_[:, k, :], start=(k == 0), stop=(k == 3),
                               reads=[skpc] + [("uT", c) for c in range(4)], writes=[psk(bb + 2)])
                        for b3 in range(3):
                            I("dve", "tensor_tensor", out=tmg[b3], in0=ps(bb + b3), in1=gTv[:, b3 * 8 + n, :], op=ALU.mult, reads=[psk(bb + b3), ("gT", b3 * 8 + n)], writes=[("tmg", b3)])
                        I("pool", "tensor_tensor", out=tmg[0], in0=tmg[0], in1=tmg[1], op=ALU.add, reads=[("tmg", 0), ("tmg", 1)], writes=[("tmg", 0)])
                        I("pool", "tensor_tensor", out=mgv[:, n, :], in0=tmg[0], in1=tmg[2], op=ALU.add, reads=[("tmg", 0), ("tmg", 2)], writes=[("mg", n)])
                for half in range(2):
                    slwo, skwo = wload(wbwo_d[l, half], ("wbwo", l, half))
                    wov = slwo.rearrange("p (k j) -> p k j", k=8)
                    for cc in range(4):
                        n = half * 4 + cc
                        bank = 6 + n % 2
                        for k in range(KD):
                            I("pe", "matmul", ps(bank), lhsT=wov[:, k, cc * 128:(cc + 1) * 128], rhs=mgv[:, k, :], start=(k == 0), stop=(k == KD - 1),
                               reads=[skwo, ("mg", k)], writes=[psk(bank)])
                        I("dve", "scalar_tensor_tensor", out=xTv[:, n, :], in0=ps(bank), scalar=DV(S1, n, s), in1=xTv[:, n, :], op0=ALU.mult, op1=ALU.add,
                           reads=[psk(bank), ("xT", n), "derv"], writes=[("xT", n)])
                if m == 0:
                    dump("mg", mg, [("mg", k) for k in range(KD)])
                    dump("z1", xT, [("xT", k) for k in range(KD)])
                layer_norm("xT", "ln1_w", "ln1_b", X1B, True)
                if m == 0:
                    dump("x1", xT, [("xT", k) for k in range(KD)])
                    dump("h2", hT, [("hT", k) for k in range(KD)])
                if stage < 9:
                    continue
                for b in range(8):
                    sl1, sk1 = wload(wb1_d[l, b], ("wb1", l, b))
                    w1v = sl1.rearrange("p (k j) -> p k j", k=8)
                    for cc in range(4):
                        n = b * 4 + cc
                        bank = nb(0, 4)
                        for k in range(KD):
                            I("pe", "matmul", ps(bank), lhsT=w1v[:, k, cc * 128:(cc + 1) * 128], rhs=hTv[:, k, :], start=(k == 0), stop=(k == KD - 1),
                               reads=[sk1, ("hT", k)], writes=[psk(bank)])
                        rb, rk = rl[n % 2], ("rl", n % 2)
                        I("act", "activation", out=rb, in_=ps(bank), func=AF.Relu, bias=V("b1", n), scale=1.0, reads=[psk(bank), "vec"], writes=[rk])
                        I("pool", "tensor_tensor", out=hidv[:, n, :], in0=rb, in1=rb, op=ALU.mult, reads=[rk], writes=[("hid", n), ("gT", n) if n < 24 else ("sz", n - 24)])
                for n in range(8):
                    sl2, sk2 = wload(wb2_d[l, n], ("wb2", l, n))
                    w2v = sl2.rearrange("p (k j) -> p k j", k=32)
                    bank = 4 + n % 2
                    for k in range(32):
                        I("pe", "matmul", ps(bank), lhsT=w2v[:, k, :], rhs=hidv[:, k, :], start=(k == 0), stop=(k == 31), reads=[sk2, ("hid", k)], writes=[psk(bank)])
                    I("dve", "scalar_tensor_tensor", out=xTv[:, n, :], in0=ps(bank), scalar=DV(S2, n, s), in1=xTv[:, n, :], op0=ALU.mult, op1=ALU.add,
                       reads=[psk(bank), ("xT", n), "derv"], writes=[("xT", n)])
                def store_chunk(k, t0=t0, m=m):
                    I("sp", "dma_start", out=xT_d[s, :, k, t0:t0 + 512], in_=xTv[:, k, :], reads=[("xT", k)], writes=[("xTd", s, m)], dma="xTst%d" % k)
                layer_norm("xT", "ln2_w", "ln2_b", None, False, after_chunk=(None if last else store_chunk))
                if last:
                    for tt in range(4):
                        yb, yk = ytok[tt % 2], ("ytok", tt % 2)
                        for kq in range(2):
                            bank = nb(0, 4)
                            for kk in range(4):
                                k = kq * 4 + kk
                                I("pe", "transpose", out=ps(bank)[:, kk * 128:(kk + 1) * 128], in_=xTv[:, k, tt * 128:(tt + 1) * 128], identity=ident_f,
                                   reads=[("xT", k), "ident_f"], writes=[psk(bank)])
                            if kq == 0:
                                I("dve", "tensor_copy", out=yb[:, 0:512], in_=ps(bank), reads=[psk(bank)], writes=[yk] + ZK)
                            else:
                                I("act", "activation", out=yb[:, 512:1024], in_=ps(bank), func=AF.Copy, reads=[psk(bank)], writes=[yk] + ZK)
                        I("sp", "dma_start", out=y_d[s, t0 + tt * 128:t0 + (tt + 1) * 128, :], in_=yb, reads=[yk] + ZK, writes=[("yd", s, m, tt)], dma="yst%d" % (tt % 2))
            AR.release()

        run_casts()
        for l in range(NL):
            if stage >= 1:
                layer_init(l)
                dump("modT", modT, ["modT"])
                dump("derv", derv, ["derv"])
            for s in range(NSEQ):
                if stage >= 2:
                    phase_m(l, s)
                if stage >= 5:
                    phase_main(l, s, last=(l == NL - 1))
        S_.barrier(final=True)
        print("arena high-water", AR.hw, "of", ARENA_W)
        S_.emit(nc)
    return nc


def _fm(v, k):
    return np.ascontiguousarray(np.asarray(v, np.float32).reshape(k, 128).T)


def prep_shared(inp):
    L = DEPTH
    vec = np.zeros((L, 128, NV), np.float32)
    rowv = np.zeros((L, 1, NR), np.float32)
    for l in range(L):
        b_in = inp["b_in"][l]
        def put(name, arr):
            vec[l, :, VEC[name]:VEC[name] + arr.shape[1]] = arr
        put("b_xm", _fm(b_in[OFF_MX:OFF_MX + 1024], 8))
        put("b_zm", _fm(b_in[OFF_MZ:OFF_MZ + 1024], 8))
        put("b_qa", _fm(b_in[OFF_AQ:OFF_AQ + 512], 4))
        bka = b_in[OFF_AK:OFF_AK + 128].reshape(2, 64)
        bk4 = np.zeros((128, 4), np.float32)
        for g_ in range(2):
            for hf_ in range(2):
                bk4[hf_ * 64:(hf_ + 1) * 64, g_ * 2 + hf_] = bka[g_]
        put("b_ka", bk4)
        put("b_u", _fm(b_in[OFF_C:OFF_C + 512], 4))
        put("b_g", _fm(b_in[OFF_G:OFF_G + 3072], 24))
        cw = inp["m_conv_w"][l]
        put("conv_w", np.ascontiguousarray(cw.reshape(5, 8, 128).transpose(2, 1, 0).reshape(128, 40)))
        put("conv_b", _fm(inp["m_conv_b"][l], 8))
        put("norm_w", _fm(inp["m_norm_w"][l], 8))
        put("ln1_w", _fm(inp["ln1_w"][l], 8))
        put("ln1_b", _fm(inp["ln1_b"][l], 8))
        put("ln2_w", _fm(inp["ln2_w"][l], 8))
        put("ln2_b", _fm(inp["ln2_b"][l], 8))
        put("b1", _fm(inp["mlp_b1"][l], 32))
        put("b2", _fm(inp["mlp_b2"][l], 8))
        put("ada_b", _fm(inp["ada_b"][l], 48))
        rowv[l, 0, 0:16] = b_in[OFF_MG:OFF_MG + 16]
        rowv[l, 0, 16:144] = b_in[OFF_AV:OFF_AV + 128]
        rowv[l, 0, ROW["b_vc"]:ROW["b_vc"] + 512] = b_in[OFF_C + 512:OFF_C + 1024]
        rowv[l, 0, ROW["c_ln_w"]:ROW["c_ln_w"] + 512] = inp["c_ln_w"][l]
        rowv[l, 0, ROW["c_ln_b"]:ROW["c_ln_b"] + 512] = inp["c_ln_b"][l]
        rowv[l, 0, ROW["a_sink"]:ROW["a_sink"] + 8] = inp["a_sink"][l]
    wsT = np.ascontiguousarray(np.asarray(inp["c_ws"], np.float32).transpose(0, 3, 1, 2).reshape(L, 128, 512))
    bsrow = np.ascontiguousarray(np.asarray(inp["c_bs"], np.float32).reshape(L, 1, 512))
    shared = dict(vec=vec, rowv=rowv, wsT=wsT, bsrow=bsrow)
    for k in ("ada_w", "w_in", "m_wq", "m_wk", "m_wv", "p_m", "p_a", "p_c", "w_out", "mlp_w1", "mlp_w2"):
        shared[k] = np.ascontiguousarray(np.asarray(inp[k], np.float32))
    return shared


def make_core_inputs(xs, cs, shared):
    nseq = xs.shape[0]
    cT = np.ascontiguousarray(cs.reshape(nseq, KD, 128).transpose(2, 1, 0).reshape(128, KD * nseq))
    d = dict(shared)
    d["x"] = np.ascontiguousarray(xs, dtype=np.float32)
    d["cT"] = cT.astype(np.float32)
    return d


_CACHE = {}


def kernel(**inputs):
    inp = {k: np.asarray(v) for k, v in inputs.items()}
    xp, xsm = inp["x_prompt"], inp["x_sample"]
    cp, csm = inp["c_prompt"], inp["c_sample"]
    B, S, _ = xp.shape
    BS = xsm.shape[0]
    npc, nsc = B // NCORES, BS // NCORES
    NSEQ = npc + nsc
    shared = prep_shared(inp)
    in_maps = []
    for c in range(NCORES):
        xs = np.concatenate([xp[c * npc:(c + 1) * npc], xsm[c * nsc:(c + 1) * nsc]], axis=0)
        cs = np.concatenate([cp[c * npc:(c + 1) * npc], csm[c * nsc:(c + 1) * nsc]], axis=0)
        in_maps.append(make_core_inputs(xs, cs, shared))
    key = (NSEQ, S)
    if key not in _CACHE:
        _CACHE[key] = build_program(NSEQ, S)
    nc = _CACHE[key]
    res = run_bass_kernel_spmd(nc, in_maps, core_ids=list(range(NCORES)))
    yp = np.empty((B, S, D), np.float32)
    ys = np.empty((BS, S, D), np.float32)
    for c in range(NCORES):
        y = res.results[c]["y"]
        yp[c * npc:(c + 1) * npc] = y[:npc]
        ys[c * nsc:(c + 1) * nsc] = y[npc:]
    return (yp, ys)
```

```python
import contextlib
import numpy as np
import concourse.bass as bass
import concourse.mybir as mybir
from concourse.bass_utils import run_bass_kernel_spmd

F32 = mybir.dt.float32
BF16 = mybir.dt.bfloat16
AF = mybir.ActivationFunctionType
ALU = mybir.AluOpType

D = 1024
KD = 8
DEPTH = 2
NCORES = 8
M_HEADS = 4
M_DH = 256
A_HEADS = 8
A_DH = 64
D_A = 512
D_C = 512
D_FF = 4096
OFF_MX = 0
OFF_MZ = 1024
OFF_MG = 2048
OFF_AQ = 2064
OFF_AK = 2576
OFF_AV = 2704
OFF_C = 2832
OFF_G = 3856
N_IN = 6928
ALPHA = (2 * DEPTH) ** 0.25
LN_EPS = 1e-5
EPS_P = LN_EPS / (ALPHA * ALPHA)
NEG = -240000.0

ENGS = ("pe", "act", "dve", "pool", "sp")

VEC = {}
_o = 0
for _n, _k in (("b_xm", 8), ("b_zm", 8), ("b_qa", 4), ("b_ka", 4), ("b_u", 4), ("b_g", 24), ("conv_w", 40),
               ("conv_b", 8), ("norm_w", 8), ("ln1_w", 8), ("ln1_b", 8), ("ln2_w", 8), ("ln2_b", 8),
               ("b1", 32), ("b2", 8), ("ada_b", 48)):
    VEC[_n] = _o
    _o += _k
NV = _o
ROW = {"b_gv": 0, "b_vc": 144, "c_ln_w": 656, "c_ln_b": 1168, "a_sink": 1680}
NR = 1688


class Sched:
    def __init__(self):
        self.ops = []
        self.last_writer = {}
        self.readers = {}
        self.eng_ops = {e: [] for e in ENGS}
        self.last_dma = {}

    def op(self, eng, fn, reads=(), writes=(), dma=None, extra_deps=(), soft_deps=()):
        idx = len(self.ops)
        deps = set(extra_deps) | set(soft_deps)
        for k in reads:
            w = self.last_writer.get(k)
            if w is not None:
                deps.add(w)
        for k in writes:
            w = self.last_writer.get(k)
            if w is not None:
                deps.add(w)
            for r in self.readers.get(k, ()):
                deps.add(r)
        deps.discard(idx)
        if eng == "pe" and dma is None:
            deps = {d for d in deps if self.ops[d]["eng"] != "pe" or self.ops[d]["dma"] is not None}
        o = dict(eng=eng, fn=fn, deps=deps, dma=dma, signal=False, cnt=None, semkey=None, soft=set(soft_deps))
        self.ops.append(o)
        self.eng_ops[eng].append(idx)
        for k in reads:
            self.readers.setdefault(k, []).append(idx)
        for k in writes:
            self.last_writer[k] = idx
            self.readers[k] = []
        if dma is not None:
            self.last_dma[dma] = idx
        return idx

    def I(self, eng, name, *args, reads=(), writes=(), dma=None, **kw):
        return self.op(eng, (lambda h, name=name, args=args, kw=kw: getattr(h, name)(*args, **kw)), reads=reads, writes=writes, dma=dma)

    def barrier(self, final=False):
        lasts = set()
        for e in ENGS:
            for idx in reversed(self.eng_ops[e]):
                if self.ops[idx]["dma"] is None and self.ops[idx]["fn"] is not None:
                    lasts.add(idx)
                    break
        for k, idx in self.last_dma.items():
            if final or not k.startswith("wc_"):
                lasts.add(idx)
        for e in ENGS:
            self.op(e, None, extra_deps=lasts)
        keep = {k: v for k, v in self.last_writer.items()
                if self.ops[v]["dma"] is not None and self.ops[v]["dma"].startswith("wc_")}
        self.last_writer = keep
        self.readers = {}

    def emit(self, nc):
        ops = self.ops
        for o in ops:
            for d in o["deps"]:
                ops[d]["signal"] = True
            if o["dma"] is not None:
                o["signal"] = True
        counters = {}
        for e in ENGS:
            for idx in self.eng_ops[e]:
                o = ops[idx]
                if not o["signal"]:
                    continue
                if o["dma"] is not None:
                    key = ("dma", o["dma"])
                    counters[key] = counters.get(key, 0) + 16
                else:
                    key = ("eng", e)
                    counters[key] = counters.get(key, 0) + 1
                o["semkey"] = key
                o["cnt"] = counters[key]
        semkeys = list(counters.keys())
        self.n_sems = len(semkeys)
        print("n_sems", len(semkeys), "n_ops", len(ops))
        sems = {}
        with contextlib.ExitStack() as stack:
            for i, k in enumerate(semkeys):
                sems[k] = stack.enter_context(nc.semaphore("s%d" % i))
            block = stack.enter_context(nc.Block())

            def run_engine(e, handle):
                waited = {}
                for idx in self.eng_ops[e]:
                    o = ops[idx]
                    need = {}
                    for d in o["deps"]:
                        po = ops[d]
                        k = po["semkey"]
                        c_ = po["cnt"]
                        if po["dma"] is not None and po["dma"].startswith("wc_") and d not in o["soft"]:
                            c_ = counters[k]
                        if c_ > need.get(k, 0):
                            need[k] = c_
                    for k, v in need.items():
                        if waited.get(k, 0) < v:
                            handle.wait_ge(sems[k], v)
                            waited[k] = v
                    if o["fn"] is None:
                        continue
                    ins = o["fn"](handle)
                    if o["signal"]:
                        ins.then_inc(sems[o["semkey"]], 16 if o["dma"] is not None else 1)

            @block.tensor
            def _(h):
                run_engine("pe", h)

            @block.scalar
            def _(h):
                run_engine("act", h)

            @block.vector
            def _(h):
                run_engine("dve", h)

            @block.gpsimd
            def _(h):
                run_engine("pool", h)

            @block.sync
            def _(h):
                run_engine("sp", h)


class Arena:
    def __init__(self, ap, nwords):
        self.ap = ap
        self.n = nwords
        self.off = 0
        self.marks = []

    def f32(self, n):
        a = self.ap[:, self.off:self.off + n]
        self.off += n
        self.hw = max(getattr(self, "hw", 0), self.off)
        assert self.off <= self.n, "arena overflow %d > %d" % (self.off, self.n)
        return a

    def bf(self, n):
        w = (n + 1) // 2
        return self.f32(w).bitcast(BF16)[:, 0:n]

    def mark(self):
        self.marks.append(self.off)

    def release(self):
        self.off = self.marks.pop()


def build_program(NSEQ, S, NL=DEPTH, debug=(), stage=9):
    NT = S // 128
    NMT = S // 512
    nc = bass.Bass("TRN2", target_bir_lowering=False)
    dt = nc.dram_tensor

    x_d = dt("x", [NSEQ, S, D], F32, kind="ExternalInput").ap()
    cT_d = dt("cT", [128, KD * NSEQ], F32, kind="ExternalInput").ap()
    ada_d = dt("ada_w", [DEPTH, D, 6 * D], F32, kind="ExternalInput").ap()
    win_d = dt("w_in", [DEPTH, D, N_IN], F32, kind="ExternalInput").ap()
    wq_d = dt("m_wq", [DEPTH, 4, 256, 256], F32, kind="ExternalInput").ap()
    wk_d = dt("m_wk", [DEPTH, 4, 256, 256], F32, kind="ExternalInput").ap()
    wv_d = dt("m_wv", [DEPTH, 4, 256, 256], F32, kind="ExternalInput").ap()
    pm_d = dt("p_m", [DEPTH, D, D], F32, kind="ExternalInput").ap()
    pa_d = dt("p_a", [DEPTH, D_A, D], F32, kind="ExternalInput").ap()
    pc_d = dt("p_c", [DEPTH, D_C, D], F32, kind="ExternalInput").ap()
    wo_d = dt("w_out", [DEPTH, D, D], F32, kind="ExternalInput").ap()
    w1_d = dt("mlp_w1", [DEPTH, D, D_FF], F32, kind="ExternalInput").ap()
    w2_d = dt("mlp_w2", [DEPTH, D_FF, D], F32, kind="ExternalInput").ap()
    wsT_d = dt("wsT", [DEPTH, 128, 512], F32, kind="ExternalInput").ap()
    vec_d = dt("vec", [DEPTH, 128, NV], F32, kind="ExternalInput").ap()
    row_d = dt("rowv", [DEPTH, 1, NR], F32, kind="ExternalInput").ap()
    bs_d = dt("bsrow", [DEPTH, 1, 512], F32, kind="ExternalInput").ap()
    y_d = dt("y", [NSEQ, S, D], F32, kind="ExternalOutput").ap()
    dbg_d = {}
    for name, shape, dty in debug:
        dbg_d[name] = dt("dbg_" + name, list(shape), dty, kind="ExternalOutput").ap()

    def dump(name, src_ap, reads):
        if name in dbg_d:
            S_.barrier()
            S_.I("sp", "dma_start", out=dbg_d[name], in_=src_ap, reads=reads, writes=[("dbg", name)], dma="dbg_" + name)
            S_.barrier()

    NBLK_IN = 15
    wbin_d = dt("wb_in", [DEPTH, NBLK_IN, 128, 4096], BF16, kind="Internal").ap()
    wbqkv_d = dt("wb_qkv", [DEPTH, 3, 128, 2048], BF16, kind="Internal").ap()
    wbpm_d = dt("wb_pm", [DEPTH, 2, 128, 4096], BF16, kind="Internal").ap()
    wbwo_d = dt("wb_wo", [DEPTH, 2, 128, 4096], BF16, kind="Internal").ap()
    wbpa_d = dt("wb_pa", [DEPTH, 128, 4096], BF16, kind="Internal").ap()
    wbpc_d = dt("wb_pc", [DEPTH, 128, 4096], BF16, kind="Internal").ap()
    wb1_d = dt("wb_1", [DEPTH, 8, 128, 4096], BF16, kind="Internal").ap()
    wb2_d = dt("wb_2", [DEPTH, 8, 128, 4096], BF16, kind="Internal").ap()
    wbws_d = dt("wb_ws", [DEPTH, 128, 512], BF16, kind="Internal").ap()
    xT_d = dt("xT_scr", [NSEQ, 128, KD, S], F32, kind="Internal").ap()
    hf_d = dt("hfwd_scr", [NSEQ, S, D], F32, kind="Internal").ap()
    hn_d = dt("hnT_scr", [NSEQ, 128, KD, S], BF16, kind="Internal").ap()

    INBLK = {"xm0": (0, OFF_MX), "xm1": (1, OFF_MX + 512), "zm0": (2, OFF_MZ), "zm1": (3, OFF_MZ + 512),
             "qa": (4, OFF_AQ), "u": (5, OFF_C), "vc": (6, OFF_C + 512)}
    for i in range(6):
        INBLK["g%d" % i] = (7 + i, OFF_G + 512 * i)
    BLK_KA, BLK_GV = 13, 14

    S_ = Sched()
    op = S_.op
    I = S_.I

    with contextlib.ExitStack() as es:
        ARENA_W = 52736
        arena_t = es.enter_context(nc.sbuf_tensor("arena", [128, ARENA_W], F32))
        AR = Arena(arena_t[:, :], ARENA_W)
        PS = [es.enter_context(nc.psum_tensor("ps%d" % i, [128, 512], F32)) for i in range(8)]

        def ps(i):
            return PS[i][:, :]

        def psk(i):
            return ("ps", i)

        cast_blocks = []

        def v3(k):
            return lambda t: t.rearrange("p (k j) -> p k j", k=k)

        for l in range(DEPTH if stage >= 0 else 0):
            wsrc = win_d[l].rearrange("(k p) n -> p k n", p=128)
            for name, (b, c0) in INBLK.items():
                cast_blocks.append((wbin_d[l, b], 4096, [(v3(8), wsrc[:, :, c0:c0 + 512])]))
            pcs = []
            for g in range(2):
                for hf in range(2):
                    pcs.append(((lambda t, g=g, hf=hf: t.rearrange("p (k v j) -> p k v j", k=8, v=4)[:, :, g * 2 + hf, hf * 64:(hf + 1) * 64]),
                                wsrc[:, :, OFF_AK + 64 * g:OFF_AK + 64 * g + 64]))
            cast_blocks.append((wbin_d[l, BLK_KA], 4096, pcs, True))
            cast_blocks.append((wbin_d[l, BLK_GV, :, 0:1152], 1152, [
                ((lambda t: t[:, 0:1152].rearrange("p (k j) -> p k j", k=8)[:, :, 0:16]), wsrc[:, :, OFF_MG:OFF_MG + 16]),
                ((lambda t: t[:, 0:1152].rearrange("p (k j) -> p k j", k=8)[:, :, 16:144]), wsrc[:, :, OFF_AV:OFF_AV + 128])]))
            for i, wd in enumerate((wq_d, wk_d, wv_d)):
                cast_blocks.append((wbqkv_d[l, i], 2048, [((lambda t: t[:, 0:2048].rearrange("p (h c e) -> p h c e", h=4, c=2)), wd[l].rearrange("h (c p) e -> p h c e", p=128))]))
            for b in range(2):
                cast_blocks.append((wbpm_d[l, b], 4096, [(v3(8), pm_d[l].rearrange("(k p) n -> p k n", p=128)[:, :, b * 512:(b + 1) * 512])]))
                cast_blocks.append((wbwo_d[l, b], 4096, [(v3(8), wo_d[l].rearrange("(k p) n -> p k n", p=128)[:, :, b * 512:(b + 1) * 512])]))
            cast_blocks.append((wbpa_d[l], 4096, [(v3(4), pa_d[l].rearrange("(k p) n -> p k n", p=128))]))
            cast_blocks.append((wbpc_d[l], 4096, [(v3(4), pc_d[l].rearrange("(k p) n -> p k n", p=128))]))
            for b in range(8):
                cast_blocks.append((wb1_d[l, b], 4096, [(v3(8), w1_d[l].rearrange("(k p) n -> p k n", p=128)[:, :, b * 512:(b + 1) * 512])]))
            src2 = w2_d[l].rearrange("(k p) n -> p k n", p=128)
            for b in range(8):
                pcs = []
                for kk in range(4):
                    pcs.append(((lambda t, kk=kk: t.rearrange("p (k j) -> p k j", k=32)[:, kk * 8:(kk + 1) * 8, :]), src2[:, kk * 8:(kk + 1) * 8, b * 128:(b + 1) * 128]))
                cast_blocks.append((wb2_d[l, b], 4096, pcs))
            cast_blocks.append((wbws_d[l], 512, [((lambda t: t[:, 0:512]), wsT_d[l])]))

        ident_f = AR.f32(128)
        ident_b = AR.bf(128)
        ones_b = AR.bf(128)
        ones_f = AR.f32(128)
        Umask = AR.f32(128)
        Lmask = AR.f32(128)
        abias = AR.bf(3 * 2 * 512)
        tmpc = AR.f32(128)
        tmpc2 = AR.f32(128)
        I("pool", "memset", ident_f, 0.0, writes=["ident_f"])
        I("pool", "affine_select", out=ident_f, in_=ident_f, pattern=[[-1, 128]], compare_op=ALU.not_equal, fill=1.0,
                                             base=0, channel_multiplier=1, reads=["ident_f"], writes=["ident_f"])
        I("dve", "tensor_copy", out=ident_b, in_=ident_f, reads=["ident_f"], writes=["ident_b"])
        I("pool", "memset", ones_f, 1.0, writes=["ones_f"])
        I("pool", "memset", ones_b, 1.0, writes=["ones_b"])
        I("pool", "affine_select", out=Umask, in_=ones_f, pattern=[[1, 128]], compare_op=ALU.is_ge, fill=0.0,
                                             base=0, channel_multiplier=-1, reads=["ones_f"], writes=["Umask"])
        I("pool", "affine_select", out=Lmask, in_=ones_f, pattern=[[-1, 128]], compare_op=ALU.is_ge, fill=0.0,
                                             base=0, channel_multiplier=1, reads=["ones_f"], writes=["Lmask"])
        abv = abias.rearrange("p (o g r q) -> p o g r q", o=3, g=2, r=4)
        for oi, o in enumerate((-1, 0, 1)):
            I("pool", "iota", tmpc, pattern=[[1, 128]], base=-128 * o, channel_multiplier=-1,
                                             allow_small_or_imprecise_dtypes=True, writes=["tmpc"])
            for hh in range(8):
                g, r = hh // 4, hh % 4
                dst = abv[:, oi, g, r, :]
                cst = -(2.0 ** (2 - hh))
                if o == 0:
                    I("dve", "tensor_scalar", out=tmpc2, in0=tmpc, scalar1=cst, scalar2=None, op0=ALU.mult, reads=["tmpc"], writes=["tmpc2"])
                    I("dve", "scalar_tensor_tensor", out=dst, in0=tmpc, scalar=-cst, in1=tmpc2, op0=ALU.mult, op1=ALU.min, reads=["tmpc", "tmpc2"], writes=["abias"])
                else:
                    I("dve", "tensor_scalar", out=dst, in0=tmpc, scalar1=(cst if o == -1 else -cst), scalar2=None, op0=ALU.mult, reads=["tmpc"], writes=["abias"])
                if o == -1:
                    I("pool", "affine_select", out=dst, in_=dst, pattern=[[-1, 128]], compare_op=ALU.is_ge, fill=NEG,
                                                                  base=0, channel_multiplier=1, reads=["abias"], writes=["abias"])
                elif o == 1:
                    I("pool", "affine_select", out=dst, in_=dst, pattern=[[1, 128]], compare_op=ALU.is_ge, fill=NEG,
                                                                  base=0, channel_multiplier=-1, reads=["abias"], writes=["abias"])

        vec = AR.f32(NV)
        rowbc = AR.f32(NR)
        bsrow = AR.f32(512)
        esink = AR.f32(8)
        wsb = AR.bf(512)
        cT = AR.f32(KD * NSEQ)
        scT = AR.f32(KD * NSEQ)
        modT = AR.f32(48 * NSEQ)
        derv = AR.f32(7 * KD * NSEQ)
        modv = modT.rearrange("p (c s) -> p c s", s=NSEQ)
        dvv = derv.rearrange("p (i k s) -> p i k s", i=7, k=KD)
        A1, B1, S1, W2P, B2P, S2, X1B = range(7)
        kaT2 = AR.bf(4 * S)
        va = AR.bf(NT * 2 * 66)
        kaT2v = kaT2.rearrange("p (g s) -> p g s", g=4)
        vav = va.rearrange("p (t g j) -> p t g j", t=NT, g=2)

        def V(name, c=0, n=1):
            o = VEC[name] + c
            return vec[:, o:o + n]

        I("sp", "dma_start", out=cT, in_=cT_d, writes=["cT"], dma="cT")
        I("act", "activation", out=scT, in_=cT, func=AF.Silu, reads=["cT"], writes=["scT"])

        AR.mark()

        def run_casts():
            AR.mark()
            NB = 4
            Fst = [AR.f32(4096) for _ in range(NB)]
            Bst = [AR.bf(4096) for _ in range(NB)]
            n = len(cast_blocks)
            for bi in range(n + 2):
                if bi < n:
                    blk = cast_blocks[bi]
                    i = bi % NB
                    for vf, sap in blk[2]:
                        I("sp", "dma_start", out=vf(Fst[i]), in_=sap, writes=[("cF", i)], dma="cF%d" % i)
                bj = bi - 2
                if bj >= 0:
                    blk = cast_blocks[bj]
                    dst, W, pcs = blk[0], blk[1], blk[2]
                    i = bj % NB
                    F, B = Fst[i], Bst[i]
                    if len(blk) > 3:
                        I("pool", "memset", B, 0.0, writes=[("cB", i)])
                    for vf, sap in pcs:
                        if bj % 2 == 0:
                            I("act", "activation", out=vf(B), in_=vf(F), func=AF.Copy, reads=[("cF", i)], writes=[("cB", i)])
                        else:
                            I("dve", "tensor_copy", out=vf(B), in_=vf(F), reads=[("cF", i)], writes=[("cB", i)])
                    I("sp", "dma_start", out=dst, in_=B[:, 0:W], reads=[("cB", i)], writes=[("wb", bj)], dma="cB%d" % i)
            AR.release()
            S_.barrier()

        def layer_init(l):
            S_.barrier()
            I("sp", "dma_start", out=vec, in_=vec_d[l], writes=["vec"], dma="tab_vec")
            I("sp", "dma_start", out=rowbc, in_=row_d[l, 0].partition_broadcast(128), writes=["rowbc"], dma="tab_row")
            I("sp", "dma_start", out=bsrow[0:1, :], in_=bs_d[l], writes=["bsrow"], dma="tab_bs")
            I("sp", "dma_start", out=wsb, in_=wbws_d[l], reads=[("wbws", l)], writes=["wsb"], dma="tab_ws")
            I("act", "activation", out=esink, in_=rowbc[:, ROW["a_sink"]:ROW["a_sink"] + 8], func=AF.Exp, reads=["rowbc"], writes=["esink"])
            AR.mark()
            NSLOT = 4
            slots = [AR.f32(KD * 128) for _ in range(NSLOT)]
            scv = scT.rearrange("p (k s) -> p k s", k=KD)
            for c in range(48):
                sl = slots[c % NSLOT]
                slv = sl.rearrange("p (k j) -> p k j", k=KD)
                src = ada_d[l].rearrange("(k p) n -> p k n", p=128)[:, :, c * 128:(c + 1) * 128]
                I("sp", "dma_start", out=slv, in_=src, writes=[("adas", c % NSLOT)], dma="adas%d" % (c % NSLOT))
                bank = c % 2
                for k in range(KD):
                    I("pe", "matmul", ps(bank)[:, 0:NSEQ], lhsT=slv[:, k, :], rhs=scv[:, k, :], start=(k == 0), stop=(k == KD - 1),
                       reads=[("adas", c % NSLOT), "scT"], writes=[psk(bank)])
                I("act", "activation", out=modv[:, c, :], in_=ps(bank)[:, 0:NSEQ], func=AF.Identity, bias=V("ada_b", c), scale=1.0,
                   reads=[psk(bank), "vec"], writes=["modT"])
            def bc(name):
                o = VEC[name]
                return vec[:, o:o + KD].rearrange("p (k o) -> p k o", o=1).to_broadcast([128, KD, NSEQ])
            sh1, sc1, g1, sh2, sc2, g2 = (modv[:, i * 8:(i + 1) * 8, :] for i in range(6))
            dv = lambda i: dvv[:, i, :, :]
            R, W = ["modT", "vec", "derv"], ["derv"]
            I("dve", "tensor_scalar", out=dv(A1), in0=sc1, scalar1=1.0, scalar2=None, op0=ALU.add, reads=R, writes=W)
            I("dve", "tensor_copy", out=dv(B1), in_=sh1, reads=R, writes=W)
            I("dve", "tensor_scalar", out=dv(S1), in0=g1, scalar1=1.0, scalar2=1.0 / ALPHA, op0=ALU.add, op1=ALU.mult, reads=R, writes=W)
            I("dve", "tensor_scalar", out=dv(S2), in0=g2, scalar1=1.0, scalar2=1.0 / ALPHA, op0=ALU.add, op1=ALU.mult, reads=R, writes=W)
            I("dve", "tensor_scalar", out=dv(W2P), in0=sc2, scalar1=1.0, scalar2=None, op0=ALU.add, reads=R, writes=W)
            I("dve", "tensor_tensor", out=dv(B2P), in0=dv(W2P), in1=bc("ln1_b"), op=ALU.mult, reads=R, writes=W)
            I("dve", "tensor_tensor", out=dv(B2P), in0=dv(B2P), in1=sh2, op=ALU.add, reads=R, writes=W)
            I("dve", "tensor_tensor", out=dv(W2P), in0=dv(W2P), in1=bc("ln1_w"), op=ALU.mult, reads=R, writes=W)
            I("dve", "tensor_tensor", out=dv(X1B), in0=dv(S2), in1=bc("b2"), op=ALU.mult, reads=R, writes=W)
            I("dve", "tensor_tensor", out=dv(X1B), in0=dv(X1B), in1=bc("ln1_b"), op=ALU.add, reads=R, writes=W)
            AR.release()

        def DV(i, k, s):
            return dvv[:, i, k, s:s + 1]

        def phase_m(l, s):
            S_.barrier()
            AR.mark()
            xmT = AR.bf(KD * (S + 4))
            xcT = AR.bf(KD * S)
            xmv = xmT.rearrange("p (k t) -> p k t", k=KD)
            xcv = xcT.rearrange("p (k t) -> p k t", k=KD)
            gates = AR.f32(NT * 16)
            lsq = AR.f32(NT * 8)
            aneg = AR.f32(NT * 8)
            Aneg = AR.f32(NT * 8)
            bsc = AR.f32(NT * 8)
            bsck = AR.f32(NT * 8)
            ena = AR.f32(NT * 8)
            eA = AR.f32(NT * 8)
            gtmp = AR.f32(NT * 8)
            gv3 = gates.rearrange("p (t j) -> p t j", j=16)
            T8 = lambda a: a.rearrange("p (t j) -> p t j", j=8)
            wqkv = AR.bf(3 * 2048)
            wqkvv = wqkv.rearrange("p (i h c e) -> p i h c e", i=3, h=4, c=2)
            I("sp", "dma_start", out=wqkv.rearrange("p (i n) -> p i n", i=3), in_=wbqkv_d[l].rearrange("i p n -> p i n"),
               reads=[("wbqkv", l)], writes=["wqkv"], dma="wqkv")
            I("pool", "memset", xmv[:, :, 0:2], 0.0, writes=["xmT"])
            I("pool", "memset", xmv[:, :, S + 2:S + 4], 0.0, writes=["xmT"])
            I("pool", "memset", vav[:, :, :, 64:66], 1.0, writes=["va"])

            AR.mark()
            wxm = [AR.bf(4096), AR.bf(4096)]
            wka = AR.bf(4096)
            wgv = AR.bf(1152)
            for b in range(2):
                I("sp", "dma_start", out=wxm[b], in_=wbin_d[l, b], reads=[("wbin", l, b)], writes=[("wxm", b)], dma="wpre%d" % b)
            I("sp", "dma_start", out=wka, in_=wbin_d[l, BLK_KA], reads=[("wbin", l, BLK_KA)], writes=["wka"], dma="wpreka")
            I("sp", "dma_start", out=wgv, in_=wbin_d[l, BLK_GV, :, 0:1152], reads=[("wbin", l, BLK_GV)], writes=["wgv"], dma="wpregv")
            wxmv = [w.rearrange("p (k j) -> p k j", k=8) for w in wxm]
            wkav = wka.rearrange("p (k g j) -> p k g j", k=8, g=4)
            wgvv = wgv.rearrange("p (k j) -> p k j", k=8)
            xT = AR.f32(KD * 512)
            xTv = xT.rearrange("p (k t) -> p k t", k=KD)
            hT = AR.bf(KD * 512)
            hTv = hT.rearrange("p (k t) -> p k t", k=KD)
            xtok = [AR.f32(1024), AR.f32(1024)] if l == 0 else None
            pbank = [0]

            def nextbank(lo=0, hi=4):
                b = lo + pbank[0] % (hi - lo)
                pbank[0] += 1
                return b

            for m in range(NMT):
                t0 = m * 512
                if l == 0:
                    for tt in range(4):
                        xb = xtok[tt % 2]
                        I("sp", "dma_start", out=xb, in_=x_d[s, t0 + tt * 128:t0 + (tt + 1) * 128, :],
                           writes=[("xtok", tt % 2)], dma="xtok%d" % (tt % 2))
                        for kq in range(2):
                            bank = nextbank()
                            for kk in range(4):
                                k = kq * 4 + kk
                                I("pe", "transpose", out=ps(bank)[:, kk * 128:(kk + 1) * 128], in_=xb[:, k * 128:(k + 1) * 128], identity=ident_f,
                                   reads=[("xtok", tt % 2), "ident_f"], writes=[psk(bank)])
                            eng = "dve" if kq == 0 else "act"
                            dst = xTv[:, kq * 4:(kq + 1) * 4, tt * 128:(tt + 1) * 128]
                            srcp = ps(bank).rearrange("p (k t) -> p k t", k=4)
                            if eng == "dve":
                                I("dve", "tensor_copy", out=dst, in_=srcp, reads=[psk(bank)], writes=["xT"])
                            else:
                                I("act", "activation", out=dst, in_=srcp, func=AF.Copy, reads=[psk(bank)], writes=["xT"])
                    I("sp", "dma_start", out=xT_d[s, :, :, t0:t0 + 512], in_=xTv, reads=["xT"], writes=[("xTd", s, m)], dma="xTst")
                else:
                    I("sp", "dma_start", out=xTv, in_=xT_d[s, :, :, t0:t0 + 512], reads=[("xTd", s, m)], writes=["xT"], dma="xTld")
                for k in range(KD):
                    I("act", "activation", out=hTv[:, k, :], in_=xTv[:, k, :], func=AF.Identity, scale=DV(A1, k, s), bias=DV(B1, k, s),
                       reads=["xT", "derv"], writes=["hT"])
                for c in range(8):
                    bank = nextbank()
                    w = wxmv[c // 4]
                    for k in range(KD):
                        I("pe", "matmul", ps(bank), lhsT=w[:, k, (c % 4) * 128:(c % 4 + 1) * 128], rhs=hTv[:, k, :], start=(k == 0), stop=(k == KD - 1),
                           reads=["hT", ("wxm", c // 4)], writes=[psk(bank)])
                    I("act", "activation", out=xmv[:, c, 2 + t0:2 + t0 + 512], in_=ps(bank), func=AF.Identity, bias=V("b_xm", c), scale=1.0,
                       reads=[psk(bank), "vec"], writes=["xmT"])
                for g in range(4):
                    bank = nextbank()
                    for k in range(KD):
                        I("pe", "matmul", ps(bank), lhsT=wkav[:, k, g, :], rhs=hTv[:, k, :], start=(k == 0), stop=(k == KD - 1),
                           reads=["hT", "wka"], writes=[psk(bank)])
                    I("act", "activation", out=kaT2v[:, g, t0:t0 + 512], in_=ps(bank), func=AF.Identity, bias=V("b_ka", g), scale=1.0,
                       reads=[psk(bank), "vec"], writes=["kaT2"])
                for tt in range(4):
                    t = m * 4 + tt
                    bank = 4 + tt % 2
                    for k in range(KD):
                        I("pe", "matmul", ps(bank)[:, 0:144], lhsT=hTv[:, k, tt * 128:(tt + 1) * 128], rhs=wgvv[:, k, :], start=(k == 0), stop=(k == KD - 1),
                           reads=["hT", "wgv"], writes=[psk(bank)])
                    I("dve", "tensor_tensor", out=gv3[:, t, :], in0=ps(bank)[:, 0:16], in1=rowbc[:, 0:16], op=ALU.add,
                       reads=[psk(bank), "rowbc"], writes=["gates"])
                    I("dve", "tensor_tensor", out=vav[:, t, :, 0:64], in0=ps(bank)[:, 16:144].rearrange("p (g j) -> p g j", g=2),
                                                                        in1=rowbc[:, 16:144].rearrange("p (g j) -> p g j", g=2), op=ALU.add,
                       reads=[psk(bank), "rowbc"], writes=["va"])
            AR.release()
            S_.barrier()
            dump("xmT", xmT, ["xmT"])
            dump("kaT2", kaT2, ["kaT2"])
            dump("va", va, ["va"])
            dump("gates", gates, ["gates"])
            if stage < 3:
                AR.release()
                return

            AR.mark()
            dgs = AR.bf(40 * 128)
            dgv = dgs.rearrange("p (c j) -> p c j", c=40)
            for cj in range(40):
                I("dve", "tensor_scalar", out=dgv[:, cj, :], in0=ident_f, scalar1=V("conv_w", cj), scalar2=None, op0=ALU.mult, reads=["ident_f", "vec"], writes=[("dgs", cj)])
            cb = 0
            for c in range(KD):
                for q in range(S // 512):
                    bank = cb % 4
                    cb += 1
                    for j in range(5):
                        I("pe", "matmul", ps(bank), lhsT=dgv[:, c * 5 + j, :], rhs=xmv[:, c, q * 512 + j:q * 512 + j + 512], start=(j == 0), stop=(j == 4),
                          reads=[("dgs", c * 5 + j), "xmT"], writes=[psk(bank)])
                    I("act", "activation", out=xcv[:, c, q * 512:(q + 1) * 512], in_=ps(bank), func=AF.Silu, bias=V("conv_b", c), scale=1.0, reads=[psk(bank), "vec"], writes=["xcT"])
            gvd = gates.rearrange("p (t d j) -> p t d j", d=2, j=8)
            l4 = lsq.rearrange("p (t d j) -> p t d j", d=2, j=4)
            I("act", "activation", out=l4, in_=gvd[:, :, :, 4:8], func=AF.Exp, scale=-1.0, reads=["gates"], writes=["lsq"])
            I("act", "activation", out=lsq, in_=lsq, func=AF.Ln, bias=1.0, scale=1.0, reads=["lsq"], writes=["lsq"])
            NTJ = NT * 8
            I("pe", "matmul", ps(6)[:, 0:NTJ], lhsT=Umask, rhs=lsq, start=True, stop=True, reads=["lsq", "Umask"], writes=[psk(6)])
            I("pe", "matmul", ps(6)[:, NTJ:2 * NTJ], lhsT=Lmask, rhs=lsq, start=True, stop=True, reads=["lsq", "Lmask"], writes=[psk(6)])
            I("pe", "matmul", ps(7)[:, 0:NTJ], lhsT=ones_f, rhs=lsq, start=True, stop=True, reads=["lsq", "ones_f"], writes=[psk(7)])
            a4 = aneg.rearrange("p (t d j) -> p t d j", d=2, j=4)
            pu = ps(6)[:, 0:NTJ].rearrange("p (t d j) -> p t d j", d=2, j=4)
            pl = ps(6)[:, NTJ:2 * NTJ].rearrange("p (t d j) -> p t d j", d=2, j=4)
            I("dve", "tensor_copy", out=a4[:, :, 0, :], in_=pu[:, :, 0, :], reads=[psk(6)], writes=["aneg"])
            I("dve", "tensor_copy", out=a4[:, :, 1, :], in_=pl[:, :, 1, :], reads=[psk(6)], writes=["aneg"])
            I("dve", "tensor_copy", out=Aneg, in_=ps(7)[:, 0:NTJ], reads=[psk(7)], writes=["Aneg"])
            g4 = gtmp.rearrange("p (t d j) -> p t d j", d=2, j=4)
            I("dve", "tensor_tensor", out=g4, in0=gvd[:, :, :, 0:4], in1=a4, op=ALU.add, reads=["gates", "aneg"], writes=["gtmp"])
            I("act", "activation", out=bsc, in_=gtmp, func=AF.Exp, reads=["gtmp"], writes=["bsc"])
            I("act", "activation", out=bsck, in_=bsc, func=AF.Copy, scale=1.0 / 16.0, reads=["bsc"], writes=["bsck"])
            I("act", "activation", out=ena, in_=aneg, func=AF.Exp, reads=["aneg"], writes=["ena"])
            I("act", "activation", out=eA, in_=Aneg, func=AF.Exp, scale=-1.0, reads=["Aneg"], writes=["eA"])
            AR.release()
            S_.barrier()
            dump("xcT", xcT, ["xcT"])
            dump("bsc", bsc, ["bsc"])
            dump("ena", ena, ["ena"])
            dump("eA", eA, ["eA"])
            if stage < 4:
                AR.release()
                return

            AR.mark()
            qT = AR.bf(KD * 512)
            kT = AR.bf(KD * 512)
            qTv = qT.rearrange("p (k t) -> p k t", k=KD)
            kTv = kT.rearrange("p (k t) -> p k t", k=KD)
            kraw = [AR.bf(1024) for _ in range(4)]
            vtok = [AR.bf(4 * 258) for _ in range(4)]
            ktl = [AR.bf(1024), AR.bf(1024)]
            Sp = [AR.bf(512), AR.bf(512)]
            CTb = AR.bf(4 * 2 * 258)
            Cv = CTb.rearrange("p (h c j) -> p h c j", h=4, c=2)
            rr = AR.f32(8)
            htile = [AR.f32(1024), AR.f32(1024)]
            hfl = [AR.f32(1024), AR.f32(1024)]
            hnts = [AR.bf(1024), AR.bf(1024)]
            hnT = [AR.bf(1024), AR.bf(1024)]
            bnst = AR.f32(4 * 6)
            bnag = AR.f32(4 * 2)
            lnt = AR.f32(8)
            for i in range(4):
                I("pool", "memset", vtok[i].rearrange("p (h j) -> p h j", h=4)[:, :, 256:258], 1.0, writes=[("vtok", i)])
            ti = 0
            ucnt = 0
            pending = []
            for d in range(2):
                I("pool", "memset", CTb, 0.0, writes=[("CTb", hh_) for hh_ in range(4)])
                mask = Umask if d == 0 else Lmask
                mlist = range(NMT) if d == 0 else range(NMT - 1, -1, -1)
                for m in mlist:
                    t0 = m * 512
                    pcnt = 0
                    for which, dstv, sc in ((0, qTv, 1.0), (1, kTv, 1.0 / 16.0)):
                        for hh in range(4):
                            for ec in range(2):
                                bank = pcnt % 4
                                pcnt += 1
                                for dc in range(2):
                                    I("pe", "matmul", ps(bank), lhsT=wqkvv[:, which, hh, dc, ec * 128:(ec + 1) * 128], rhs=xcv[:, hh * 2 + dc, t0:t0 + 512], start=(dc == 0), stop=(dc == 1),
                                      reads=["wqkv", "xcT"], writes=[psk(bank)])
                                dd = dstv[:, hh * 2 + ec, :]
                                wk_ = "qT" if which == 0 else "kT"
                                if ec == 0:
                                    I("act", "activation", out=dd, in_=ps(bank), func=AF.Copy, scale=sc, reads=[psk(bank)], writes=[wk_])
                                else:
                                    I("dve", "tensor_scalar", out=dd, in0=ps(bank), scalar1=sc, scalar2=None, op0=ALU.mult, reads=[psk(bank)], writes=[wk_])
                    for tt in range(4):
                        g0 = (m * 4 + tt) * 128
                        for which in (1, 2):
                            src_ = xcv if which == 1 else xmv
                            off = g0 if which == 1 else g0 + 2
                            for hp in range(2):
                                bank = pcnt % 4
                                pcnt += 1
                                for hi in range(2):
                                    hh = hp * 2 + hi
                                    for dc in range(2):
                                        I("pe", "matmul", ps(bank)[:, hi * 256:(hi + 1) * 256], lhsT=src_[:, hh * 2 + dc, off:off + 128], rhs=wqkvv[:, which, hh, dc, :], start=(dc == 0), stop=(dc == 1),
                                          reads=["wqkv", "xcT" if which == 1 else "xmT"], writes=[psk(bank)])
                                if which == 1:
                                    I("act", "activation", out=kraw[tt][:, hp * 512:(hp + 1) * 512], in_=ps(bank), func=AF.Copy, reads=[psk(bank)], writes=[("kraw", tt)])
                                else:
                                    I("dve", "tensor_copy", out=vtok[tt].rearrange("p (h j) -> p h j", h=4)[:, hp * 2:hp * 2 + 2, 0:256], in_=ps(bank).rearrange("p (h e) -> p h e", h=2),
                                      reads=[psk(bank)], writes=[("vtok", tt)])
                    tlist = range(4) if d == 0 else range(3, -1, -1)
                    for tt in tlist:
                        t = m * 4 + tt
                        tc0 = tt * 128
                        g0 = t * 128
                        kb_, sb_ = ktl[ti % 2], Sp[ti % 2]
                        kk_, sk_ = ("ktl", ti % 2), ("Sp", ti % 2)
                        vk_ = ("vtok", tt)
                        hb_, hk_ = htile[ti % 2], ("htile", ti % 2)
                        fb_, fk_ = hfl[ti % 2], ("hfl", ti % 2)
                        nb_, nk_ = hnT[ti % 2], ("hnT", ti % 2)
                        par = ti % 2
                        ti += 1
                        kbv = kb_.rearrange("p (h e) -> p h e", h=4)
                        vbv = vtok[tt].rearrange("p (h j) -> p h j", h=4)
                        sbv = sb_.rearrange("p (h q) -> p h q", h=4)
                        krv = kraw[tt].rearrange("p (h e) -> p h e", h=4)
                        if d == 1:
                            I("sp", "dma_start", out=fb_, in_=hf_d[s, g0:g0 + 128, :], reads=[("hfd", s, t)], writes=[fk_], dma="hfl%d" % par)
                        for hh in range(4):
                            for ec in range(2):
                                I("pe", "matmul", ps(4)[:, hh * 128:(hh + 1) * 128], lhsT=kTv[:, hh * 2 + ec, tc0:tc0 + 128], rhs=qTv[:, hh * 2 + ec, tc0:tc0 + 128],
                                  start=(ec == 0), stop=(ec == 1), reads=["qT", "kT"], writes=[psk(4)])
                        for hh in range(4):
                            I("act", "activation", out=kbv[:, hh, :], in_=krv[:, hh, :], func=AF.Copy, scale=T8(bsck)[:, t, d * 4 + hh:d * 4 + hh + 1],
                              reads=[("kraw", tt), "bsck"], writes=[kk_])
                        for hh in range(4):
                            I("dve", "scalar_tensor_tensor", out=sbv[:, hh, :], in0=ps(4)[:, hh * 128:(hh + 1) * 128], scalar=T8(bsc)[:, t, d * 4 + hh:d * 4 + hh + 1], in1=mask, op0=ALU.mult, op1=ALU.mult,
                              reads=[psk(4), "bsc", "Umask", "Lmask"], writes=[sk_])
                        for hh in range(4):
                            nbank = hh
                            I("pe", "matmul", ps(nbank)[:, 0:257], lhsT=sbv[:, hh, :], rhs=vbv[:, hh, 0:257], start=True, stop=False, reads=[sk_, vk_], writes=[psk(nbank)])
                            for ec in range(2):
                                I("pe", "matmul", ps(nbank)[:, 0:257], lhsT=qTv[:, hh * 2 + ec, tc0:tc0 + 128], rhs=Cv[:, hh, ec, 0:257], start=False, stop=(ec == 1),
                                  reads=["qT", ("CTb", hh)], writes=[psk(nbank)])
                        for hh in range(4):
                            for ec in range(2):
                                ubank = 5 + ucnt % 2
                                ucnt += 1
                                I("pe", "matmul", ps(ubank)[:, 0:257], lhsT=kbv[:, hh, ec * 128:(ec + 1) * 128], rhs=vbv[:, hh, 0:257], start=True, stop=False, reads=[kk_, vk_], writes=[psk(ubank)])
                                I("pe", "matmul", ps(ubank)[:, 0:257], lhsT=ident_b, rhs=Cv[:, hh, ec, 0:257], start=False, stop=True, reads=["ident_b", ("CTb", hh)], writes=[psk(ubank)])
                                I("act", "activation", out=Cv[:, hh, ec, 0:257], in_=ps(ubank)[:, 0:257], func=AF.Copy, scale=T8(eA)[:, t, d * 4 + hh:d * 4 + hh + 1],
                                  reads=[psk(ubank), "eA"], writes=[("CTb", hh)])
                        for hh in range(4):
                            nbank = hh
                            rc = rr[:, hh:hh + 1]
                            I("dve", "tensor_scalar", out=rc, in0=ps(nbank)[:, 256:257], scalar1=T8(ena)[:, t, d * 4 + hh:d * 4 + hh + 1], scalar2=None, op0=ALU.max,
                              reads=[psk(nbank), "ena"], writes=[("rr", hh)])
                            I("dve", "scalar_tensor_tensor", out=rc, in0=ps(nbank)[:, 256:257], scalar=-1.0, in1=rc, op0=ALU.mult, op1=ALU.max,
                              reads=[psk(nbank), ("rr", hh)], writes=[("rr", hh)])
                            I("dve", "reciprocal", out=rc, in_=rc, reads=[("rr", hh)], writes=[("rr", hh)])
                            hdst = hb_[:, hh * 256:(hh + 1) * 256]
                            if d == 0:
                                I("act", "activation", out=hdst, in_=ps(nbank)[:, 0:256], func=AF.Copy, scale=rc, reads=[psk(nbank), ("rr", hh)], writes=[hk_])
                            else:
                                I("dve", "scalar_tensor_tensor", out=hdst, in0=ps(nbank)[:, 0:256], scalar=rc, in1=fb_[:, hh * 256:(hh + 1) * 256], op0=ALU.mult, op1=ALU.add,
                                  reads=[psk(nbank), ("rr", hh), fk_], writes=[hk_])
                        if d == 0:
                            I("sp", "dma_start", out=hf_d[s, g0:g0 + 128, :], in_=hb_, reads=[hk_], writes=[("hfd", s, t)], dma="hfst%d" % par)
                        else:
                            if pending:
                                pending.pop()()
                            hnt = hnts[par]
                            for hh in range(4):
                                I("dve", "bn_stats", out=bnst[:, hh * 6:(hh + 1) * 6], in_=hb_[:, hh * 256:(hh + 1) * 256], reads=[hk_], writes=[("bnst", hh)])
                                I("dve", "bn_aggr", out=bnag[:, hh * 2:(hh + 1) * 2], in_=bnst[:, hh * 6:(hh + 1) * 6], reads=[("bnst", hh)], writes=["bnag"])
                            bv = bnag.rearrange("p (h j) -> p h j", j=2)
                            I("act", "activation", out=lnt[:, 0:4], in_=bv[:, :, 1], func=AF.Ln, bias=LN_EPS, scale=1.0, reads=["bnag"], writes=["lnt"])
                            I("act", "activation", out=lnt[:, 0:4], in_=lnt[:, 0:4], func=AF.Exp, scale=-0.5, reads=["lnt"], writes=["lnt"])
                            I("dve", "scalar_tensor_tensor", out=lnt[:, 4:8], in0=bv[:, :, 0], scalar=-1.0, in1=lnt[:, 0:4], op0=ALU.mult, op1=ALU.mult, reads=["bnag", "lnt"], writes=["lnt"])
                            for hh in range(4):
                                I("act", "activation", out=hnt[:, hh * 256:(hh + 1) * 256], in_=hb_[:, hh * 256:(hh + 1) * 256], func=AF.Identity, scale=lnt[:, hh:hh + 1], bias=lnt[:, 4 + hh:5 + hh],
                                  reads=[hk_, "lnt"], writes=[("hnt", par)])
                            def finish(hnt=hnt, nb_=nb_, nk_=nk_, g0=g0, t=t, par=par):
                                pst = ps(7).bitcast(BF16)
                                for k in range(KD):
                                    I("pe", "transpose", out=pst[:, k * 128:(k + 1) * 128], in_=hnt[:, k * 128:(k + 1) * 128], identity=ident_b, reads=[("hnt", par), "ident_b"], writes=[psk(7)])
                                nw = vec[:, VEC["norm_w"]:VEC["norm_w"] + 8].rearrange("p (k o) -> p k o", o=1).to_broadcast([128, 8, 128])
                                I("dve", "tensor_tensor", out=nb_.rearrange("p (k t) -> p k t", k=8), in0=pst.rearrange("p (k t) -> p k t", k=8), in1=nw, op=ALU.mult, reads=[psk(7), "vec"], writes=[nk_])
                                I("sp", "dma_start", out=hn_d[s, :, :, g0:g0 + 128], in_=nb_.rearrange("p (k t) -> p k t", k=8), reads=[nk_], writes=[("hnd", s, t)], dma="hnst%d" % par)
                            pending.append(finish)
            if pending:
                pending.pop()()
            AR.release()
            AR.release()
            dump("hnT", hn_d[s], [])
            dump("hf", hf_d[s], [])

        def phase_main(l, s, last):
            S_.barrier()
            AR.mark()
            NSLOT = 4
            wslot = [AR.bf(4096) for _ in range(NSLOT)]
            wctr = [0]

            def wload(src, key):
                i = wctr[0] % NSLOT
                wctr[0] += 1
                sl = wslot[i]
                I("sp", "dma_start", out=sl, in_=src, reads=[key], writes=[("wslot", i)], dma="wsl%d" % i)
                return sl, ("wslot", i)

            xT = AR.f32(KD * 512)
            xTv = xT.rearrange("p (k t) -> p k t", k=KD)
            hT = AR.bf(KD * 512)
            hTv = hT.rearrange("p (k t) -> p k t", k=KD)
            hid = AR.bf(32 * 512)
            hidv = hid.rearrange("p (k t) -> p k t", k=32)
            gTv = hidv[:, 0:24, :]
            szv = hidv[:, 24:32, :]
            ymT = AR.bf(KD * 512)
            ymv = ymT.rearrange("p (k t) -> p k t", k=KD)
            qaT = AR.bf(4 * 512)
            qav = qaT.rearrange("p (k t) -> p k t", k=4)
            uT = AR.bf(4 * 512)
            uv_ = uT.rearrange("p (k t) -> p k t", k=4)
            vln = [AR.bf(512) for _ in range(4)]
            vcf = [AR.f32(512) for _ in range(4)]
            pT = [AR.bf(3 * 512), AR.bf(3 * 512)]
            yat = AR.bf(512)
            yaT = AR.bf(4 * 512)
            yav = yaT.rearrange("p (k t) -> p k t", k=4)
            tmg = [AR.f32(512) for _ in range(3)]
            mg = AR.bf(KD * 512)
            mgv = mg.rearrange("p (k t) -> p k t", k=KD)
            zsq = AR.bf(KD * 512)
            zsv = zsq.rearrange("p (k t) -> p k t", k=KD)
            mean = AR.f32(512)
            rstd = AR.f32(512)
            stt = AR.f32(512)
            rl = [AR.f32(512), AR.f32(512)]
            sm8 = AR.f32(16)
            bnst = AR.f32(6)
            bnag = AR.f32(2)
            lnt = AR.f32(2)
            zsq_f = zsq.bitcast(F32)
            ytok = [zsq_f[:, 0:1024], zsq_f[:, 1024:2048]] if last else None
            ZK = [("zsq", k) for k in range(KD)]
            pb = [0]

            def nb(lo, hi):
                b = lo + pb[0] % (hi - lo)
                pb[0] += 1
                return b

            def layer_norm(zkey, wname, bname, extra_bias_idx, make_h2):
                for k in range(KD):
                    I("act", "activation", out=mgv[:, k, :], in_=xTv[:, k, :], func=AF.Copy, reads=[("xT", k)], writes=[("mg", k)])
                    I("pool", "tensor_tensor", out=zsv[:, k, :], in0=xTv[:, k, :], in1=xTv[:, k, :], op=ALU.mult, reads=[("xT", k)], writes=[("zsq", k)])
                for k in range(KD):
                    I("pe", "matmul", ps(0), lhsT=ones_b, rhs=mgv[:, k, :], start=(k == 0), stop=(k == KD - 1), reads=[("mg", k), "ones_b"], writes=[psk(0)])
                for k in range(KD):
                    I("pe", "matmul", ps(1), lhsT=ones_b, rhs=zsv[:, k, :], start=(k == 0), stop=(k == KD - 1), reads=[("zsq", k), "ones_b"], writes=[psk(1)])
                I("act", "activation", out=mean, in_=ps(0), func=AF.Copy, scale=1.0 / D, reads=[psk(0)], writes=["mean"])
                I("dve", "tensor_tensor", out=stt, in0=mean, in1=mean, op=ALU.mult, reads=["mean"], writes=["stt"])
                I("dve", "scalar_tensor_tensor", out=stt, in0=ps(1), scalar=1.0 / D, in1=stt, op0=ALU.mult, op1=ALU.subtract, reads=[psk(1), "stt"], writes=["stt"])
                I("dve", "tensor_scalar", out=stt, in0=stt, scalar1=0.0, scalar2=EPS_P, op0=ALU.max, op1=ALU.add, reads=["stt"], writes=["stt"])
                I("act", "activation", out=rstd, in_=stt, func=AF.Ln, reads=["stt"], writes=["rstd"])
                I("act", "activation", out=rstd, in_=rstd, func=AF.Exp, scale=-0.5, reads=["rstd"], writes=["rstd"])
                for k in range(KD):
                    I("dve", "tensor_tensor", out=xTv[:, k, :], in0=xTv[:, k, :], in1=mean, op=ALU.subtract, reads=[("xT", k), "mean"], writes=[("xT", k)])
                    I("pool", "tensor_tensor", out=xTv[:, k, :], in0=xTv[:, k, :], in1=rstd, op=ALU.mult, reads=[("xT", k), "rstd"], writes=[("xT", k)])
                    if make_h2:
                        I("act", "activation", out=hTv[:, k, :], in_=xTv[:, k, :], func=AF.Identity, scale=DV(W2P, k, s), bias=DV(B2P, k, s),
                           reads=[("xT", k), "derv"], writes=[("hT", k)])
                        I("act", "activation", out=xTv[:, k, :], in_=xTv[:, k, :], func=AF.Identity, scale=V(wname, k), bias=DV(X1B, k, s),
                           reads=[("xT", k), "derv", "vec"], writes=[("xT", k)])
                    else:
                        I("act", "activation", out=xTv[:, k, :], in_=xTv[:, k, :], func=AF.Identity, scale=V(wname, k), bias=V(bname, k),
                           reads=[("xT", k), "vec"], writes=[("xT", k)])

            for m in range(NMT):
                t0 = m * 512
                I("sp", "dma_start", out=xTv, in_=xT_d[s, :, :, t0:t0 + 512], reads=[("xTd", s, m)], writes=[("xT", k) for k in range(KD)], dma="xTld")
                I("sp", "dma_start", out=ymv, in_=hn_d[s, :, :, t0:t0 + 512], reads=[("hnd", s, m * 4 + i) for i in range(4)], writes=["ymT"], dma="hnld")
                for k in range(KD):
                    I("act", "activation", out=hTv[:, k, :], in_=xTv[:, k, :], func=AF.Identity, scale=DV(A1, k, s), bias=DV(B1, k, s),
                       reads=[("xT", k), "derv"], writes=[("hT", k)])
                HT = [("hT", k) for k in range(KD)]

                def proj_fm(blkname, nchunks, evac):
                    b, _ = INBLK[blkname]
                    sl, sk = wload(wbin_d[l, b], ("wbin", l, b))
                    slv = sl.rearrange("p (k j) -> p k j", k=8)
                    for c in range(nchunks):
                        bank = nb(0, 4)
                        for k in range(KD):
                            I("pe", "matmul", ps(bank), lhsT=slv[:, k, c * 128:(c + 1) * 128], rhs=hTv[:, k, :], start=(k == 0), stop=(k == KD - 1),
                               reads=[sk, ("hT", k)], writes=[psk(bank)])
                        evac(c, bank)

                for half in range(2):
                    def ev_zm(c, bank, half=half):
                        cc = half * 4 + c
                        I("act", "activation", out=szv[:, cc, :], in_=ps(bank), func=AF.Sigmoid, bias=V("b_zm", cc), scale=1.0, reads=[psk(bank), "vec"], writes=[("sz", cc), ("hid", 24 + cc)])
                    proj_fm("zm%d" % half, 4, ev_zm)

                def ev_qa(c, bank):
                    I("act", "activation", out=qav[:, c, :], in_=ps(bank), func=AF.Identity, bias=V("b_qa", c), scale=1.0, reads=[psk(bank), "vec"], writes=["qaT"])
                proj_fm("qa", 4, ev_qa)

                def ev_u(c, bank):
                    I("act", "activation", out=uv_[:, c, :], in_=ps(bank), func=AF.Gelu_apprx_tanh, bias=V("b_u", c), scale=1.0, reads=[psk(bank), "vec"], writes=[("uT", c)])
                proj_fm("u", 4, ev_u)

                for gi in range(6):
                    def ev_g(c, bank, gi=gi):
                        cc = gi * 4 + c
                        I("act", "activation", out=gTv[:, cc, :], in_=ps(bank), func=AF.Sigmoid, bias=V("b_g", cc), scale=1.0, reads=[psk(bank), "vec"], writes=[("gT", cc), ("hid", cc)])
                    proj_fm("g%d" % gi, 4, ev_g)

                for k in range(KD):
                    I("pool", "tensor_tensor", out=ymv[:, k, :], in0=ymv[:, k, :], in1=szv[:, k, :], op=ALU.mult, reads=["ymT", ("sz", k)], writes=["ymT"])

                slvc, skvc = wload(wbin_d[l, INBLK["vc"][0]], ("wbin", l, INBLK["vc"][0]))
                slvcv = slvc.rearrange("p (k j) -> p k j", k=8)
                wsv = wsb.rearrange("p (g t) -> p g t", g=4)
                for tt in range(4):
                    if stage < 6:
                        break
                    t = m * 4 + tt
                    c0 = tt * 128
                    vb, vbk = vln[tt], ("vln", tt)
                    vf, vfk = vcf[tt], ("vcf", tt)
                    vbank = 4 + tt % 2
                    for k in range(KD):
                        I("pe", "matmul", ps(vbank), lhsT=hTv[:, k, c0:c0 + 128], rhs=slvcv[:, k, :], start=(k == 0), stop=(k == KD - 1), reads=[skvc, ("hT", k)], writes=[psk(vbank)])
                    I("dve", "tensor_tensor", out=vf, in0=ps(vbank), in1=rowbc[:, ROW["b_vc"]:ROW["b_vc"] + 512], op=ALU.add, reads=[psk(vbank), "rowbc"], writes=[vfk])

                def vc_chain(tt):
                    vb, vbk = vln[tt], ("vln", tt)
                    vf, vfk = vcf[tt], ("vcf", tt)
                    I("act", "activation", out=vf, in_=vf, func=AF.Gelu_apprx_tanh, reads=[vfk], writes=[vfk])
                    I("dve", "bn_stats", out=bnst, in_=vf, reads=[vfk], writes=["bnst"])
                    I("dve", "bn_aggr", out=bnag, in_=bnst, reads=["bnst"], writes=["bnag"])
                    I("act", "activation", out=lnt[:, 0:1], in_=bnag[:, 1:2], func=AF.Ln, bias=LN_EPS, scale=1.0, reads=["bnag"], writes=["lnt"])
                    I("act", "activation", out=lnt[:, 0:1], in_=lnt[:, 0:1], func=AF.Exp, scale=-0.5, reads=["lnt"], writes=["lnt"])
                    I("dve", "scalar_tensor_tensor", out=lnt[:, 1:2], in0=bnag[:, 0:1], scalar=-1.0, in1=lnt[:, 0:1], op0=ALU.mult, op1=ALU.mult, reads=["bnag", "lnt"], writes=["lnt"])
                    I("act", "activation", out=vf, in_=vf, func=AF.Identity, scale=lnt[:, 0:1], bias=lnt[:, 1:2], reads=[vfk, "lnt"], writes=[vfk])
                    I("pool", "tensor_tensor", out=vf, in0=vf, in1=rowbc[:, ROW["c_ln_w"]:ROW["c_ln_w"] + 512], op=ALU.mult, reads=[vfk, "rowbc"], writes=[vfk])
                    I("pool", "tensor_tensor", out=vb, in0=vf, in1=rowbc[:, ROW["c_ln_b"]:ROW["c_ln_b"] + 512], op=ALU.add, reads=[vfk, "rowbc"], writes=[vbk])
                def spatial(tt):
                    c0 = tt * 128
                    vb, vbk = vln[tt], ("vln", tt)
                    sbank = 4 + tt % 2
                    for gg in range(4):
                        I("pe", "matmul", ps(sbank)[:, gg * 128:(gg + 1) * 128], lhsT=vb[:, gg * 128:(gg + 1) * 128], rhs=wsv[:, gg, :], start=True, stop=False,
                           reads=[vbk, "wsb"], writes=[psk(sbank)])
                        I("pe", "matmul", ps(sbank)[:, gg * 128:(gg + 1) * 128], lhsT=ones_f[0:1, :], rhs=bsrow[0:1, gg * 128:(gg + 1) * 128], start=False, stop=True,
                           reads=["bsrow", "ones_f"], writes=[psk(sbank)])
                    I("dve", "tensor_tensor", out=uv_[:, :, c0:c0 + 128], in0=ps(sbank).rearrange("p (g t) -> p g t", g=4), in1=uv_[:, :, c0:c0 + 128], op=ALU.mult,
                       reads=[psk(sbank)] + [("uT", c) for c in range(4)], writes=[("uT", c) for c in range(4)])
                for tt in range(4):
                    t = m * 4 + tt
                    c0 = tt * 128
                    offs = [o for o in (-1, 0, 1) if 0 <= t + o < NT]
                    for g in range(2):
                        pb_, pk_ = pT[g], ("pT", g)
                        pv = pb_.rearrange("p (o r q) -> p o r q", o=3, r=4)
                        for o in offs:
                            oi = o + 1
                            kt = t + o
                            bank = nb(0, 4)
                            I("pe", "matmul", ps(bank), lhsT=ident_b, rhs=abv[:, oi, g, :, :], start=True, stop=False,
                               reads=["abias", "ident_b"], writes=[psk(bank)])
                            for r in range(4):
                                hh = g * 4 + r
                                pr = (hh % 2) * 64
                                I("pe", "matmul", ps(bank)[:, r * 128:(r + 1) * 128], lhsT=kaT2v[:, g * 2 + hh % 2, kt * 128:(kt + 1) * 128],
                                  rhs=qav[:, hh // 2, c0:c0 + 128], start=False, stop=(r == 3), skip_group_check=True,
                                  reads=["kaT2", "qaT"], writes=[psk(bank)])
                            I("act", "activation", out=pv[:, oi, :, :], in_=ps(bank).rearrange("p (r q) -> p r q", r=4), func=AF.Exp, scale=0.125,
                               reads=[psk(bank)], writes=[pk_])
                        obank = 6 + g
                        for r in range(4):
                            for j, o in enumerate(offs):
                                oi = o + 1
                                kt = t + o
                                I("pe", "matmul",
                                    ps(obank)[:, r * 66:r * 66 + 65], lhsT=pv[:, oi, r, :], rhs=vav[:, kt, g, 0:65], start=(j == 0), stop=(j == len(offs) - 1),
                                   reads=[pk_, "va"], writes=[psk(obank)])
                        po = ps(obank)[:, 0:264].rearrange("p (r j) -> p r j", r=4)
                        I("dve", "tensor_tensor", out=sm8[:, g * 4:(g + 1) * 4], in0=po[:, :, 64], in1=esink[:, g * 4:(g + 1) * 4], op=ALU.add, reads=[psk(obank), "esink"], writes=[("sm8", g)])
                        I("dve", "reciprocal", out=sm8[:, g * 4:(g + 1) * 4], in_=sm8[:, g * 4:(g + 1) * 4], reads=[("sm8", g)], writes=[("sm8", g)])
                        yv = yat.rearrange("p (r j) -> p r j", j=64)
                        I("dve", "tensor_tensor", out=yv[:, g * 4:(g + 1) * 4, :], in0=po[:, :, 0:64],
                                                                               in1=sm8[:, g * 4:(g + 1) * 4].rearrange("p (r o) -> p r o", o=1).to_broadcast([128, 4, 64]), op=ALU.mult,
                           reads=[psk(obank), ("sm8", g)], writes=["yat"])
                    pst = ps(5).bitcast(BF16)
                    for k in range(4):
                        I("pe", "transpose", out=pst[:, 512 + k * 128:512 + (k + 1) * 128], in_=yat[:, k * 128:(k + 1) * 128], identity=ident_b,
                           reads=["yat", "ident_b"], writes=[psk(5)])
                    I("act", "activation", out=yav[:, :, c0:c0 + 128], in_=pst[:, 512:1024].rearrange("p (k t) -> p k t", k=4), func=AF.Copy, reads=[psk(5)], writes=["yaT"])
                    vc_chain(tt)
                    if tt >= 1:
                        spatial(tt - 1)

                spatial(3)
                if m == 0:
                    dump("ymT", ymT, ["ymT"])
                    dump("yaT", yaT, ["yaT"])
                    dump("ycT", uT, [("uT", c) for c in range(4)])
                if stage < 8:
                    continue
                slpa, skpa = wload(wbpa_d[l], ("wbpa", l))
                slpc, skpc = wload(wbpc_d[l], ("wbpc", l))
                pav = slpa.rearrange("p (k j) -> p k j", k=4)
                pcv = slpc.rearrange("p (k j) -> p k j", k=4)
                for half in range(2):
                    slpm, skpm = wload(wbpm_d[l, half], ("wbpm", l, half))
                    pmv = slpm.rearrange("p (k j) -> p k j", k=8)
                    for cc in range(4):
                        n = half * 4 + cc
                        bb = 3 * (n % 2)
                        for k in range(KD):
                            I("pe", "matmul", ps(bb), lhsT=pmv[:, k, cc * 128:(cc + 1) * 128], rhs=ymv[:, k, :], start=(k == 0), stop=(k == KD - 1),
                               reads=[skpm, "ymT"], writes=[psk(bb)])
                        for k in range(4):
                            I("pe", "matmul", ps(bb + 1), lhsT=pav[:, k, n * 128:(n + 1) * 128], rhs=yav[:, k, :], start=(k == 0), stop=(k == 3), reads=[skpa, "yaT"], writes=[psk(bb + 1)])
                        for k in range(4):
                            I("pe", "matmul", ps(bb + 2), lhsT=pcv[:, k, n * 128:(n + 1) * 128], rhs=uv_[:, k, :], start=(k == 0), stop=(k == 3),
                               reads=[skpc] + [("uT", c) for c in range(4)], writes=[psk(bb + 2)])
                        for b3 in range(3):
                            I("dve", "tensor_tensor", out=tmg[b3], in0=ps(bb + b3), in1=gTv[:, b3 * 8 + n, :], op=ALU.mult, reads=[psk(bb + b3), ("gT", b3 * 8 + n)], writes=[("tmg", b3)])
                        I("pool", "tensor_tensor", out=tmg[0], in0=tmg[0], in1=tmg[1], op=ALU.add, reads=[("tmg", 0), ("tmg", 1)], writes=[("tmg", 0)])
                        I("pool", "tensor_tensor", out=mgv[:, n, :], in0=tmg[0], in1=tmg[2], op=ALU.add, reads=[("tmg", 0), ("tmg", 2)], writes=[("mg", n)])
                for half in range(2):
                    slwo, skwo = wload(wbwo_d[l, half], ("wbwo", l, half))
                    wov = slwo.rearrange("p (k j) -> p k j", k=8)
                    for cc in range(4):
                        n = half * 4 + cc
                        bank = 6 + n % 2
                        for k in range(KD):
                            I("pe", "matmul", ps(bank), lhsT=wov[:, k, cc * 128:(cc + 1) * 128], rhs=mgv[:, k, :], start=(k == 0), stop=(k == KD - 1),
                               reads=[skwo, ("mg", k)], writes=[psk(bank)])
                        I("dve", "scalar_tensor_tensor", out=xTv[:, n, :], in0=ps(bank), scalar=DV(S1, n, s), in1=xTv[:, n, :], op0=ALU.mult, op1=ALU.add,
                           reads=[psk(bank), ("xT", n), "derv"], writes=[("xT", n)])
                if m == 0:
                    dump("mg", mg, [("mg", k) for k in range(KD)])
                    dump("z1", xT, [("xT", k) for k in range(KD)])
                layer_norm("xT", "ln1_w", "ln1_b", X1B, True)
                if m == 0:
                    dump("x1", xT, [("xT", k) for k in range(KD)])
                    dump("h2", hT, [("hT", k) for k in range(KD)])
                if stage < 9:
                    continue
                for b in range(8):
                    sl1, sk1 = wload(wb1_d[l, b], ("wb1", l, b))
                    w1v = sl1.rearrange("p (k j) -> p k j", k=8)
                    for cc in range(4):
                        n = b * 4 + cc
                        bank = nb(0, 4)
                        for k in range(KD):
                            I("pe", "matmul", ps(bank), lhsT=w1v[:, k, cc * 128:(cc + 1) * 128], rhs=hTv[:, k, :], start=(k == 0), stop=(k == KD - 1),
                               reads=[sk1, ("hT", k)], writes=[psk(bank)])
                        rb, rk = rl[n % 2], ("rl", n % 2)
                        I("act", "activation", out=rb, in_=ps(bank), func=AF.Relu, bias=V("b1", n), scale=1.0, reads=[psk(bank), "vec"], writes=[rk])
                        I("pool", "tensor_tensor", out=hidv[:, n, :], in0=rb, in1=rb, op=ALU.mult, reads=[rk], writes=[("hid", n), ("gT", n) if n < 24 else ("sz", n - 24)])
                for n in range(8):
                    sl2, sk2 = wload(wb2_d[l, n], ("wb2", l, n))
                    w2v = sl2.rearrange("p (k j) -> p k j", k=32)
                    bank = 4 + n % 2
                    for k in range(32):
                        I("pe", "matmul", ps(bank), lhsT=w2v[:, k, :], rhs=hidv[:, k, :], start=(k == 0), stop=(k == 31), reads=[sk2, ("hid", k)], writes=[psk(bank)])
                    I("dve", "scalar_tensor_tensor", out=xTv[:, n, :], in0=ps(bank), scalar=DV(S2, n, s), in1=xTv[:, n, :], op0=ALU.mult, op1=ALU.add,
                       reads=[psk(bank), ("xT", n), "derv"], writes=[("xT", n)])
                layer_norm("xT", "ln2_w", "ln2_b", None, False)
                if not last:
                    I("sp", "dma_start", out=xT_d[s, :, :, t0:t0 + 512], in_=xTv, reads=[("xT", k) for k in range(KD)], writes=[("xTd", s, m)], dma="xTst")
                else:
                    for tt in range(4):
                        yb, yk = ytok[tt % 2], ("ytok", tt % 2)
                        for kq in range(2):
                            bank = nb(0, 4)
                            for kk in range(4):
                                k = kq * 4 + kk
                                I("pe", "transpose", out=ps(bank)[:, kk * 128:(kk + 1) * 128], in_=xTv[:, k, tt * 128:(tt + 1) * 128], identity=ident_f,
                                   reads=[("xT", k), "ident_f"], writes=[psk(bank)])
                            if kq == 0:
                                I("dve", "tensor_copy", out=yb[:, 0:512], in_=ps(bank), reads=[psk(bank)], writes=[yk] + ZK)
                            else:
                                I("act", "activation", out=yb[:, 512:1024], in_=ps(bank), func=AF.Copy, reads=[psk(bank)], writes=[yk] + ZK)
                        I("sp", "dma_start", out=y_d[s, t0 + tt * 128:t0 + (tt + 1) * 128, :], in_=yb, reads=[yk] + ZK, writes=[("yd", s, m, tt)], dma="yst%d" % (tt % 2))
            AR.release()

        run_casts()
        for l in range(NL):
            if stage >= 1:
                layer_init(l)
                dump("modT", modT, ["modT"])
                dump("derv", derv, ["derv"])
            for s in range(NSEQ):
                if stage >= 2:
                    phase_m(l, s)
                if stage >= 5:
                    phase_main(l, s, last=(l == NL - 1))
        S_.barrier(final=True)
        print("arena high-water", AR.hw, "of", ARENA_W)
        S_.emit(nc)
    return nc


def _fm(v, k):
    return np.ascontiguousarray(np.asarray(v, np.float32).reshape(k, 128).T)


def prep_shared(inp):
    L = DEPTH
    vec = np.zeros((L, 128, NV), np.float32)
    rowv = np.zeros((L, 1, NR), np.float32)
    for l in range(L):
        b_in = inp["b_in"][l]
        def put(name, arr):
            vec[l, :, VEC[name]:VEC[name] + arr.shape[1]] = arr
        put("b_xm", _fm(b_in[OFF_MX:OFF_MX + 1024], 8))
        put("b_zm", _fm(b_in[OFF_MZ:OFF_MZ + 1024], 8))
        put("b_qa", _fm(b_in[OFF_AQ:OFF_AQ + 512], 4))
        bka = b_in[OFF_AK:OFF_AK + 128].reshape(2, 64)
        bk4 = np.zeros((128, 4), np.float32)
        for g_ in range(2):
            for hf_ in range(2):
                bk4[hf_ * 64:(hf_ + 1) * 64, g_ * 2 + hf_] = bka[g_]
        put("b_ka", bk4)
        put("b_u", _fm(b_in[OFF_C:OFF_C + 512], 4))
        put("b_g", _fm(b_in[OFF_G:OFF_G + 3072], 24))
        cw = inp["m_conv_w"][l]
        put("conv_w", np.ascontiguousarray(cw.reshape(5, 8, 128).transpose(2, 1, 0).reshape(128, 40)))
        put("conv_b", _fm(inp["m_conv_b"][l], 8))
        put("norm_w", _fm(inp["m_norm_w"][l], 8))
        put("ln1_w", _fm(inp["ln1_w"][l], 8))
        put("ln1_b", _fm(inp["ln1_b"][l], 8))
        put("ln2_w", _fm(inp["ln2_w"][l], 8))
        put("ln2_b", _fm(inp["ln2_b"][l], 8))
        put("b1", _fm(inp["mlp_b1"][l], 32))
        put("b2", _fm(inp["mlp_b2"][l], 8))
        put("ada_b", _fm(inp["ada_b"][l], 48))
        rowv[l, 0, 0:16] = b_in[OFF_MG:OFF_MG + 16]
        rowv[l, 0, 16:144] = b_in[OFF_AV:OFF_AV + 128]
        rowv[l, 0, ROW["b_vc"]:ROW["b_vc"] + 512] = b_in[OFF_C + 512:OFF_C + 1024]
        rowv[l, 0, ROW["c_ln_w"]:ROW["c_ln_w"] + 512] = inp["c_ln_w"][l]
        rowv[l, 0, ROW["c_ln_b"]:ROW["c_ln_b"] + 512] = inp["c_ln_b"][l]
        rowv[l, 0, ROW["a_sink"]:ROW["a_sink"] + 8] = inp["a_sink"][l]
    wsT = np.ascontiguousarray(np.asarray(inp["c_ws"], np.float32).transpose(0, 3, 1, 2).reshape(L, 128, 512))
    bsrow = np.ascontiguousarray(np.asarray(inp["c_bs"], np.float32).reshape(L, 1, 512))
    shared = dict(vec=vec, rowv=rowv, wsT=wsT, bsrow=bsrow)
    for k in ("ada_w", "w_in", "m_wq", "m_wk", "m_wv", "p_m", "p_a", "p_c", "w_out", "mlp_w1", "mlp_w2"):
        shared[k] = np.ascontiguousarray(np.asarray(inp[k], np.float32))
    return shared


def make_core_inputs(xs, cs, shared):
    nseq = xs.shape[0]
    cT = np.ascontiguousarray(cs.reshape(nseq, KD, 128).transpose(2, 1, 0).reshape(128, KD * nseq))
    d = dict(shared)
    d["x"] = np.ascontiguousarray(xs, dtype=np.float32)
    d["cT"] = cT.astype(np.float32)
    return d


_CACHE = {}


def kernel(**inputs):
    inp = {k: np.asarray(v) for k, v in inputs.items()}
    xp, xsm = inp["x_prompt"], inp["x_sample"]
    cp, csm = inp["c_prompt"], inp["c_sample"]
    B, S, _ = xp.shape
    BS = xsm.shape[0]
    npc, nsc = B // NCORES, BS // NCORES
    NSEQ = npc + nsc
    shared = prep_shared(inp)
    in_maps = []
    for c in range(NCORES):
        xs = np.concatenate([xp[c * npc:(c + 1) * npc], xsm[c * nsc:(c + 1) * nsc]], axis=0)
        cs = np.concatenate([cp[c * npc:(c + 1) * npc], csm[c * nsc:(c + 1) * nsc]], axis=0)
        in_maps.append(make_core_inputs(xs, cs, shared))
    key = (NSEQ, S)
    if key not in _CACHE:
        _CACHE[key] = build_program(NSEQ, S)
    nc = _CACHE[key]
    res = run_bass_kernel_spmd(nc, in_maps, core_ids=list(range(NCORES)))
    yp = np.empty((B, S, D), np.float32)
    ys = np.empty((BS, S, D), np.float32)
    for c in range(NCORES):
        y = res.results[c]["y"]
        yp[c * npc:(c + 1) * npc] = y[:npc]
        ys[c * nsc:(c + 1) * nsc] = y[npc:]
    return (yp, ys)
```
